# Optimizing a Trainium2 kernel written in Bass

```python
import jax, jax.numpy as jnp
from jax import lax
import numpy as np

D_MODEL = 1024
BATCH = 32
SEQ = 2048
DEPTH = 1

CHUNK = 64
Q_BLOCK = 128
SB_HEADS = 8
SB_HEAD_DIM = 64
SB_WIDTH = SB_HEADS * SB_HEAD_DIM
RW_HEADS = 8
RW_HEAD_DIM = 64
RW_WIDTH = RW_HEADS * RW_HEAD_DIM
W_LORA = 64
A_LORA = 64
G_LORA = 128
GN_EPS = RW_HEAD_DIM * 1e-5
N_BRANCH = 2
SB_COLS = 3 * SB_WIDTH
RW_COLS = 3 * RW_WIDTH + W_LORA + A_LORA + G_LORA
GATE_COLS = N_BRANCH * D_MODEL
IN_COLS = SB_COLS + RW_COLS + GATE_COLS
D_FF = ((8 * D_MODEL + 3 * 256 - 1) // (3 * 256)) * 256
RMS_EPS = 1e-6

kernel_name = "hybrid_stickbreak_rwkv7_swiglu_sandwich"


def rms_norm(x, g):
    xf = x.astype(jnp.float32)
    y = xf * lax.rsqrt(jnp.mean(xf * xf, axis=-1, keepdims=True) + RMS_EPS)
    return (y * g).astype(x.dtype)


def token_shift(p):
    return jnp.pad(p, ((0, 0), (1, 0), (0, 0)))[:, :-1]


def stick_breaking_attention(q, k, v):
    B, S, H, Dh = q.shape
    scale = Dh ** -0.5
    outs = []
    for i in range(S // Q_BLOCK):
        q0 = i * Q_BLOCK
        kv_len = q0 + Q_BLOCK
        qb = q[:, q0:kv_len]
        kb = k[:, :kv_len]
        vb = v[:, :kv_len]
        z = jnp.einsum('bqhd,bkhd->bhqk', qb, kb).astype(jnp.float32) * scale
        t_idx = q0 + jnp.arange(Q_BLOCK)[:, None]
        s_idx = jnp.arange(kv_len)[None, :]
        strict = s_idx < t_idx
        log_fail = jnp.where(strict, jax.nn.log_sigmoid(-z), 0.0)
        after = lax.cumsum(log_fail, axis=3, reverse=True) - log_fail
        w = jnp.where(strict, jnp.exp(jax.nn.log_sigmoid(z) + after), 0.0)
        outs.append(jnp.einsum('bhqk,bkhd->bqhd', w.astype(vb.dtype), vb))
    return jnp.concatenate(outs, axis=1)


def wkv7_scan(r, decay, k, v, kk, a):
    B, S, H, N = r.shape

    def to_chunks(t):
        return t.astype(jnp.float32).transpose(1, 0, 2, 3).reshape(S // CHUNK, CHUNK, B, H, N)

    def step(state, inp):
        r_t, w_t, k_t, v_t, kk_t, a_t = inp
        sa = jnp.einsum('bhvk,bhk->bhv', state, -kk_t)
        state = (state * w_t[:, :, None, :]
                 + sa[..., None] * (kk_t * a_t)[:, :, None, :]
                 + v_t[..., None] * k_t[:, :, None, :])
        return state, jnp.einsum('bhvk,bhk->bhv', state, r_t)

    def chunk_step(state, chunk_inp):
        return lax.scan(step, state, chunk_inp)

    state0 = jnp.zeros((B, H, N, N), jnp.float32)
    _, y = lax.scan(chunk_step, state0, tuple(to_chunks(t) for t in (r, decay, k, v, kk, a)))
    return y.reshape(S, B, H, N).transpose(1, 0, 2, 3)


def rwkv7_time_mix(p, mu, w0, w_up, a0, a_up, g_up, k_k, k_a, r_k, lnx_w, lnx_b):
    B, S, _ = p.shape
    H, N = RW_HEADS, RW_HEAD_DIM
    p = p + (token_shift(p) - p) * mu
    o1, o2, o3 = RW_WIDTH, 2 * RW_WIDTH, 3 * RW_WIDTH
    r, k, v = p[..., :o1], p[..., o1:o2], p[..., o2:o3]
    xw = p[..., o3:o3 + W_LORA]
    xa = p[..., o3 + W_LORA:o3 + W_LORA + A_LORA]
    xg = p[..., o3 + W_LORA + A_LORA:]
    w_raw = (w0 + jnp.tanh(xw) @ w_up).astype(jnp.float32)
    decay = jnp.exp(-jnp.exp(-jax.nn.softplus(-w_raw) - 0.5))
    a = jax.nn.sigmoid((a0 + xa @ a_up).astype(jnp.float32))
    g = jax.nn.sigmoid(xg) @ g_up
    kk = (k * k_k).astype(jnp.float32).reshape(B, S, H, N)
    kk = kk / jnp.maximum(jnp.linalg.norm(kk, axis=-1, keepdims=True), 1e-12)
    k = k.astype(jnp.float32) * (1.0 + (a - 1.0) * k_a)
    heads = lambda t: t.astype(jnp.float32).reshape(B, S, H, N)
    r_h, k_h, v_h = heads(r), heads(k), heads(v)
    y = wkv7_scan(r_h, heads(decay), k_h, v_h, kk, heads(a))
    mean = jnp.mean(y, axis=-1, keepdims=True)
    var = jnp.mean(jnp.square(y - mean), axis=-1, keepdims=True)
    y = ((y - mean) * lax.rsqrt(var + GN_EPS)).reshape(B, S, RW_WIDTH) * lnx_w + lnx_b
    bonus = (jnp.sum(r_h * k_h * r_k, axis=-1, keepdims=True) * v_h).reshape(B, S, RW_WIDTH)
    return ((y + bonus) * g).astype(p.dtype)


def setup_inputs(seed: int = 0) -> dict:
    key = jax.random.key(seed)
    ks = jax.random.split(key, 24)
    L, D = DEPTH, D_MODEL
    nrm = lambda k_, shape, s: jax.random.normal(k_, shape, jnp.float32) * s
    return {
        "x": nrm(ks[0], (BATCH, SEQ, D), 1.0),
        "norm_mix_pre": 1.0 + nrm(ks[1], (L, D), 0.1),
        "w_in": nrm(ks[2], (L, D, IN_COLS), D ** -0.5),
        "b_gate": nrm(ks[3], (L, GATE_COLS), 0.1),
        "mu_rw": jax.random.uniform(ks[4], (L, RW_COLS), jnp.float32),
        "w0": jax.random.uniform(ks[5], (L, RW_WIDTH), jnp.float32, -6.0, 0.0),
        "w_up": nrm(ks[6], (L, W_LORA, RW_WIDTH), 0.1),
        "a0": nrm(ks[7], (L, RW_WIDTH), 0.1),
        "a_up": nrm(ks[8], (L, A_LORA, RW_WIDTH), 0.5 * A_LORA ** -0.5),
        "g_up": nrm(ks[9], (L, G_LORA, RW_WIDTH), G_LORA ** -0.5),
        "k_k": 0.85 + nrm(ks[10], (L, RW_WIDTH), 0.05),
        "k_a": 1.0 + nrm(ks[11], (L, RW_WIDTH), 0.05),
        "r_k": nrm(ks[12], (L, RW_HEADS, RW_HEAD_DIM), 0.1),
        "lnx_w": 1.0 + nrm(ks[13], (L, RW_WIDTH), 0.1),
        "lnx_b": nrm(ks[14], (L, RW_WIDTH), 0.02),
        "w_sb_out": nrm(ks[15], (L, SB_WIDTH, D), SB_WIDTH ** -0.5),
        "w_rw_out": nrm(ks[16], (L, RW_WIDTH, D), RW_WIDTH ** -0.5),
        "w_o": nrm(ks[17], (L, D, D), D ** -0.5),
        "norm_mix_post": 1.0 + nrm(ks[18], (L, D), 0.1),
        "norm_ffn_pre": 1.0 + nrm(ks[19], (L, D), 0.1),
        "w_ffn_gate": nrm(ks[20], (L, D, D_FF), D ** -0.5),
        "w_ffn_up": nrm(ks[21], (L, D, D_FF), D ** -0.5),
        "w_ffn_down": nrm(ks[22], (L, D_FF, D), D_FF ** -0.5),
        "norm_ffn_post": 1.0 + nrm(ks[23], (L, D), 0.1),
    }


def reference(x, norm_mix_pre, w_in, b_gate, mu_rw, w0, w_up, a0, a_up, g_up, k_k, k_a, r_k,
              lnx_w, lnx_b, w_sb_out, w_rw_out, w_o, norm_mix_post, norm_ffn_pre,
              w_ffn_gate, w_ffn_up, w_ffn_down, norm_ffn_post):
    B, S, D = x.shape
    for l in range(DEPTH):
        h = rms_norm(x, norm_mix_pre[l])
        proj = h @ w_in[l]
        p_sb = proj[..., :SB_COLS]
        p_rw = proj[..., SB_COLS:SB_COLS + RW_COLS]
        gates = jax.nn.sigmoid(proj[..., SB_COLS + RW_COLS:] + b_gate[l])
        q = p_sb[..., :SB_WIDTH].reshape(B, S, SB_HEADS, SB_HEAD_DIM)
        k = p_sb[..., SB_WIDTH:2 * SB_WIDTH].reshape(B, S, SB_HEADS, SB_HEAD_DIM)
        v = p_sb[..., 2 * SB_WIDTH:].reshape(B, S, SB_HEADS, SB_HEAD_DIM)
        o_sb = stick_breaking_attention(q, k, v).reshape(B, S, SB_WIDTH)
        o_rw = rwkv7_time_mix(p_rw, mu_rw[l], w0[l], w_up[l], a0[l], a_up[l], g_up[l],
                              k_k[l], k_a[l], r_k[l], lnx_w[l], lnx_b[l])
        merged = (gates[..., :D] * (o_sb @ w_sb_out[l])
                  + gates[..., D:] * (o_rw @ w_rw_out[l]))
        x = x + rms_norm(merged @ w_o[l], norm_mix_post[l])
        h = rms_norm(x, norm_ffn_pre[l])
        f = (jax.nn.silu(h @ w_ffn_gate[l]) * (h @ w_ffn_up[l])) @ w_ffn_down[l]
        x = x + rms_norm(f, norm_ffn_post[l])
    return x
```

```python
import contextlib
import numpy as np
import concourse.bass as bass
import concourse.mybir as mybir
from concourse.bass_utils import run_bass_kernel_spmd

F32 = mybir.dt.float32
BF = mybir.dt.bfloat16
AF = mybir.ActivationFunctionType
ALU = mybir.AluOpType
AX = mybir.AxisListType

D = 1024
BT = 512
NCORES = 8
C0 = float(np.exp(-0.5))
EPOCH = 4000
PAIRMODE = 0


class _Rec:
    def __init__(self):
        self.call = None

    def __getattr__(self, name):
        def f(*a, **k):
            self.call = (name, a, k)
            return self
        return f


def _bind(fn):
    r = _Rec()
    fn(r)
    name, a, k = r.call
    return lambda e: getattr(e, name)(*a, **k)


class Prog:
    def __init__(self, nc, es):
        self.nc = nc
        self.es = es
        self.eng = {"pe": nc.tensor, "act": nc.scalar, "dve": nc.vector, "pool": nc.gpsimd, "sp": nc.sync}
        self.stream = {e: [] for e in self.eng}
        self.cnt = {e: 0 for e in self.eng}
        self.sems = {}
        self.dmacnt = {}
        self.lastw = {}
        self.readers = {}
        self.wabs = {e: {} for e in self.eng}
        self.wdma = {e: {} for e in self.eng}
        self.stage = 0
        self.maxstage = 99

    def sem(self, key):
        if key not in self.sems:
            self.sems[key] = self.es.enter_context(self.nc.semaphore("s_" + "_".join(map(str, key))))
        return self.sems[key]

    def _waits(self, eng, reads, writes):
        deps = []
        for k in reads:
            w = self.lastw.get(k)
            if w is not None:
                deps.append((w, "raw"))
        for k in writes:
            w = self.lastw.get(k)
            if w is not None:
                deps.append((w, "waw"))
            for r in self.readers.get(k, {}).values():
                deps.append((r, "war"))
        need_e = {}
        need_d = {}
        for pt, kind in deps:
            if pt[0] == "e":
                _, src, n = pt
                if src == eng and eng == "pe":
                    continue
                if self.wabs[eng].get(src, 0) >= n:
                    continue
                need_e[src] = max(need_e.get(src, 0), n)
            else:
                _, name, val = pt
                if self.wdma[eng].get(name, 0) >= val:
                    continue
                need_d[name] = max(need_d.get(name, 0), val)
        waits = []
        for src, n in need_e.items():
            self.wabs[eng][src] = n
            ep = (n - 1) // EPOCH
            waits.append((self.sem(("e", src, ep)), (n - 1) % EPOCH + 1))
        for name, val in need_d.items():
            self.wdma[eng][name] = val
            waits.append((self.sem(("d", name)), val))
        return waits

    def _record(self, pt, rkey, reads, writes):
        for k in reads:
            self.readers.setdefault(k, {})[rkey] = pt
        for k in writes:
            self.lastw[k] = pt
            self.readers[k] = {}

    def op(self, eng, fn, reads=(), writes=()):
        if self.stage > self.maxstage:
            return
        waits = self._waits(eng, reads, writes)
        self.cnt[eng] += 1
        n = self.cnt[eng]
        ep = (n - 1) // EPOCH
        self.stream[eng].append((waits, _bind(fn), (self.sem(("e", eng, ep)), 1)))
        self._record(("e", eng, n), eng, reads, writes)

    def mm_group(self, fns, reads=(), writes=()):
        eng = "pe"
        if self.stage > self.maxstage:
            return
        waits = self._waits(eng, reads, writes)
        self.cnt[eng] += 1
        n = self.cnt[eng]
        ep = (n - 1) // EPOCH
        for i, fn in enumerate(fns):
            last = i == len(fns) - 1
            self.stream[eng].append((waits if i == 0 else [], _bind(fn), (self.sem(("e", eng, ep)), 1) if last else None))
        self._record(("e", eng, n), eng, reads, writes)

    def dma(self, eng, out, in_, semname, reads=(), writes=()):
        if self.stage > self.maxstage:
            return
        waits = self._waits(eng, reads, writes)
        self.dmacnt[semname] = self.dmacnt.get(semname, 0) + 16
        val = self.dmacnt[semname]
        self.stream[eng].append((waits, lambda e: e.dma_start(out=out, in_=in_), (self.sem(("d", semname)), 16)))
        self._record(("d", semname, val), "d_" + semname, reads, writes)

    def final_wait_all_dma(self, eng):
        waits = []
        for name, val in self.dmacnt.items():
            waits.append((self.sem(("d", name)), val))
        self.stream[eng].append((waits, None, None))

    def fence(self, engs=("pe", "act", "dve", "pool"), dmas=()):
        pts = {e: self.cnt[e] for e in engs}
        for e in engs:
            if e == "pe":
                continue
            waits = []
            for name in dmas:
                val = self.dmacnt.get(name, 0)
                if val and self.wdma[e].get(name, 0) < val:
                    self.wdma[e][name] = val
                    waits.append((self.sem(("d", name)), val))
            for src, n in pts.items():
                if src == e or n == 0:
                    continue
                if self.wabs[e].get(src, 0) >= n:
                    continue
                self.wabs[e][src] = n
                ep = (n - 1) // EPOCH
                waits.append((self.sem(("e", src, ep)), (n - 1) % EPOCH + 1))
            if waits:
                self.stream[e].append((waits, None, None))

    def replay(self, eng, e):
        for waits, fn, sig in self.stream[eng]:
            for s, v in waits:
                e.wait_ge(s, v)
            if fn is None:
                continue
            ins = fn(e)
            if sig is not None:
                ins.then_inc(sig[0], sig[1])


def _consts():
    p = np.arange(128)
    c = {}
    c["ident"] = np.eye(128, dtype=np.float32)
    c["trineg"] = -(p[:, None] >= p[None, :]).astype(np.float32)
    c["onesneg"] = -np.ones((128, 128), np.float32)
    t = np.arange(512)
    am = np.zeros((128, 4, 512), np.float32)
    for i in range(4):
        tt = t // 128
        m = np.where(tt[None, :] < i, 0.0, np.where(tt[None, :] == i, (p[:, None] < (t % 128)[None, :]).astype(np.float32), 1.0))
        am[:, i, :] = m
    c["amask"] = am.reshape(128, 2048)
    c["blockones"] = (p[:, None] // 64 == p[None, :] // 64).astype(np.float32)
    same = (p[:, None] // 64 == p[None, :] // 64)
    maskS = (same & (p[:, None] < p[None, :])).astype(np.float32)
    maskI = (same & (p[:, None] <= p[None, :])).astype(np.float32)
    c["mask4"] = np.concatenate([maskS, maskI, maskS, maskI], axis=1)
    c["masklt"] = np.ascontiguousarray(maskS.T)
    c["cm01"] = np.stack([(p < 64), (p >= 64)], axis=1).astype(np.float32)
    c["cmask"] = np.broadcast_to((t % 64 != 0).astype(np.float32)[None, :], (128, 512)).copy()
    offs = {}
    cols = []
    o = 0
    for k, v in c.items():
        offs[k] = (o, v.shape[1])
        cols.append(v)
        o += v.shape[1]
    return np.ascontiguousarray(np.concatenate(cols, axis=1)), offs


_VEC_SPEC = [("g_pre", 8), ("g_fpre", 8), ("b_gate", 16), ("mu", 14), ("w0", 4), ("a0", 4), ("k_k", 4),
             ("k_a", 4), ("r_k", 4), ("lnx_w", 4), ("lnx_b", 4)]


def _vec_offs():
    offs = {}
    o = 0
    for k, n in _VEC_SPEC:
        offs[k] = o
        o += n
    return offs, o


def _fm(v, n):
    return np.ascontiguousarray(np.asarray(v, np.float32).reshape(n, 128).T)


def _slab_table():
    sl = []
    for j in range(3):
        sl.append(("w_in", 0, 8, j * 512, 512))
    for j in range(4):
        c0 = 1536 + j * 512
        sl.append(("w_in", 0, 8, c0, min(512, 3328 - c0)))
    for j in range(4):
        sl.append(("w_in", 0, 8, 3328 + j * 512, 512))
    sl.append(("w_sb_out", 0, 4, 0, 1024))
    sl.append(("w_rw_out", 0, 4, 0, 1024))
    for j in range(2):
        sl.append(("w_o", 0, 8, j * 512, 512))
    for j in range(11):
        sl.append(("w_gu", 0, 8, j * 256, 256))
    for hf in range(2):
        for (k0, nk) in ((0, 8), (8, 8), (16, 6)):
            sl.append(("w_down", k0, nk, hf * 512, 512))
    return sl


def build(nseq, S, dbg=False, maxstage=99):
    nblk = S // BT
    consts_np, coff = _consts()
    voff, nvec = _vec_offs()
    slabs = _slab_table()
    NSL = len(slabs)
    NCC = consts_np.shape[1]
    nc = bass.Bass("TRN2", target_bir_lowering=False)
    dt = nc.dram_tensor
    x_d = dt("x", [nseq, S, D], F32, kind="ExternalInput").ap()
    y_d = dt("y", [nseq, S, D], F32, kind="ExternalOutput").ap()
    wd = {
        "w_in": dt("w_in", [1024, 5376], F32, kind="ExternalInput").ap(),
        "w_sb_out": dt("w_sb_out", [512, 1024], F32, kind="ExternalInput").ap(),
        "w_rw_out": dt("w_rw_out", [512, 1024], F32, kind="ExternalInput").ap(),
        "w_o": dt("w_o", [1024, 1024], F32, kind="ExternalInput").ap(),
        "w_gate": dt("w_gate", [1024, 2816], F32, kind="ExternalInput").ap(),
        "w_upf": dt("w_upf", [1024, 2816], F32, kind="ExternalInput").ap(),
        "w_down": dt("w_down", [2816, 1024], F32, kind="ExternalInput").ap(),
    }
    lw_d = dt("lora_wa", [128, 512], F32, kind="ExternalInput").ap()
    lg_d = dt("lora_g", [128, 512], F32, kind="ExternalInput").ap()
    consts_d = dt("consts", [128, NCC], F32, kind="ExternalInput").ap()
    vecs_d = dt("vecs", [128, nvec], F32, kind="ExternalInput").ap()
    rows_d = dt("rows", [128, 2048], F32, kind="ExternalInput").ap()
    wsl_d = dt("wsl", [NSL, 128, 4096], BF, kind="Internal").ap()
    dbg_d = dt("dbg", [128, 4096], F32, kind="ExternalOutput").ap() if dbg else None

    es = contextlib.ExitStack()
    uid = [0]
    with es:
        P = Prog(nc, es)
        P.maxstage = maxstage

        def sbx(stack, name, shape, dtype):
            uid[0] += 1
            return stack.enter_context(nc.sbuf_tensor(f"sb{uid[0]}_{name}", shape, dtype))

        def sb(name, shape, dtype):
            return sbx(es, name, shape, dtype)

        cbf = sb("cbf", [128, NCC], BF)
        vecs = sb("vecs", [128, nvec], F32)
        rows = sb("rows", [128, 2048], F32)
        onem = sb("onem", [128, 32], F32)
        bones_f = sb("bones_f", [128, 128], F32)
        lwa_bf = sb("lwa_bf", [128, 512], BF)
        lg_bf = sb("lg_bf", [128, 512], BF)
        stat = sb("stat", [128, 32], F32)
        hT = sb("hT", [128, 8, 512], BF)
        kT = sb("kT", [128, 4, S], BF)
        vtm = sb("vtm", [128, S // 128, 512], BF)
        osbT = sb("osbT", [128, 4, 512], BF)
        orwT = sb("orwT", [128, 4, 512], BF)
        carry = sb("carry", [128, 16], F32)
        zeros_bf = sb("zeros_bf", [128, 64], BF)
        S32 = sb("S32", [128, 4, 64], F32)
        slab = [sb(f"slab{i}", [128, 4096], BF) for i in range(4)]
        psall = es.enter_context(nc.psum_tensor("psall", [128, 8, 512], F32))
        ps = [psall[:, i, :] for i in range(8)]

        def cs(name):
            o, n = coff[name]
            return cbf[:, o:o + n]

        ident_bf = cs("ident")
        trineg_bf = cs("trineg")
        onesneg_bf = cs("onesneg")
        bones_bf = cs("blockones")
        mask4 = cs("mask4")
        masklt = cs("masklt")
        cmask = cs("cmask")
        cm01 = cs("cm01")

        def amask(i):
            o, n = coff["amask"]
            return cbf[:, o + i * 512:o + (i + 1) * 512]

        def vcol(name, c):
            o = voff[name] + c
            return vecs[:, o:o + 1]

        for c0_ in range(0, NCC, 1024):
            c1_ = min(NCC, c0_ + 1024)
            P.dma("pool", cbf[:, c0_:c1_], consts_d[:, c0_:c1_], "cst", writes=["cbf"])
        P.dma("sp", vecs[:], vecs_d, "cvec", writes=["vecs"])
        P.dma("sp", rows[:], rows_d, "crow", writes=["rows"])
        P.dma("pool", lwa_bf[:], lw_d, "clwa", writes=["lwa"])
        P.dma("pool", lg_bf[:], lg_d, "clg", writes=["lg"])
        P.op("dve", lambda e: e.tensor_copy(out=bones_f[:], in_=bones_bf), reads=["cbf"], writes=["bones_f"])
        P.op("dve", lambda e: e.memset(zeros_bf[:], 0.0), writes=["zeros"])
        P.op("dve", lambda e: e.tensor_scalar(out=onem[:, 0:14], in0=vecs[:, voff["mu"]:voff["mu"] + 14], scalar1=-1.0,
                                              scalar2=1.0, op0=ALU.mult, op1=ALU.add), reads=["vecs"], writes=["onem"])
        P.op("dve", lambda e: e.tensor_scalar(out=onem[:, 16:20], in0=vecs[:, voff["k_a"]:voff["k_a"] + 4], scalar1=-1.0,
                                              scalar2=1.0, op0=ALU.mult, op1=ALU.add), reads=["vecs"], writes=["onem"])
        for si, (wn, k0, nk, c0, ncol) in enumerate(slabs):
            dst = wsl_d[si]
            if wn == "w_gu":
                for half, src in enumerate((wd["w_gate"], wd["w_upf"])):
                    o = dst.rearrange("p (k n) -> p k n", k=8)[:, :, half * 256:(half + 1) * 256]
                    i = src.rearrange("(k p) n -> p k n", p=128)[:, :, c0:c0 + ncol]
                    P.dma("pool", o, i, f"cw{si}", writes=[("wsl", si)])
            else:
                width = 4096 // nk if wn != "w_down" else 512
                o = dst[:, 0:nk * width].rearrange("p (k n) -> p k n", k=nk)[:, :, 0:ncol]
                i = wd[wn].rearrange("(k p) n -> p k n", p=128)[:, k0:k0 + nk, c0:c0 + ncol]
                P.dma("pool", o, i, f"cw{si}", writes=[("wsl", si)])

        slab_i = [0]

        def load_slab(si):
            slot = slab_i[0] % 4
            slab_i[0] += 1
            wn, k0, nk, c0, ncol = slabs[si]
            width = 512 if wn in ("w_down", "w_gu") else 4096 // nk
            used = 512 if wn == "w_gu" else ncol
            if used == width:
                P.dma("sp", slab[slot][:, 0:nk * width], wsl_d[si][:, 0:nk * width], f"sl{slot}",
                      reads=[("wsl", si)], writes=[("slab", slot)])
            else:
                P.dma("sp", slab[slot][:, 0:nk * width].rearrange("p (k n) -> p k n", k=nk)[:, :, 0:used],
                      wsl_d[si][:, 0:nk * width].rearrange("p (k n) -> p k n", k=nk)[:, :, 0:used], f"sl{slot}",
                      reads=[("wsl", si)], writes=[("slab", slot)])
            return slot

        psi = [0]

        def bank(lo=0, hi=8):
            b = lo + psi[0] % (hi - lo)
            psi[0] += 1
            return b

        def mm(out, lhsT, rhs, start=True, stop=True):
            return lambda e: e.matmul(out, lhsT=lhsT, rhs=rhs, start=start, stop=stop)

        def rstd_from_ss(col0, eps):
            P.op("act", lambda e: e.activation(out=stat[:, col0:col0 + 4], in_=stat[:, col0:col0 + 4], func=AF.Ln,
                                               scale=1.0 / D, bias=eps),
                 reads=[("stat", col0)], writes=[("stat", col0)])
            P.op("act", lambda e: e.activation(out=stat[:, col0:col0 + 4], in_=stat[:, col0:col0 + 4], func=AF.Exp, scale=-0.5),
                 reads=[("stat", col0)], writes=[("stat", col0)])

        def norm_transpose(src, skey, col0, gname, dstT, dkey, xn, junk):
            for tt in range(4):
                P.op("act", lambda e, tt=tt: e.activation(out=junk[:], in_=src[:, tt, :], func=AF.Square,
                                                          accum_out=stat[:, col0 + tt:col0 + tt + 1]),
                     reads=[(skey, tt)], writes=["junk", ("stat", col0)])
            rstd_from_ss(col0, 1e-6)
            for tt in range(4):
                P.op("dve", lambda e, tt=tt: e.tensor_scalar(out=xn[:, tt, :], in0=src[:, tt, :],
                                                             scalar1=stat[:, col0 + tt:col0 + tt + 1], scalar2=None,
                                                             op0=ALU.mult),
                     reads=[(skey, tt), ("stat", col0)], writes=[("xn", tt)])
            for c in range(8):
                b = bank()
                pst = ps[b][:].bitcast(BF)
                for tt in range(4):
                    P.op("pe", lambda e, tt=tt, c=c, pst=pst: e.transpose(out=pst[:, tt * 128:(tt + 1) * 128],
                                                                          in_=xn[:, tt, c * 128:(c + 1) * 128],
                                                                          identity=ident_bf),
                         reads=[("xn", tt), "cbf"], writes=[("ps", b)])
                P.op("act", lambda e, c=c, pst=pst: e.activation(out=dstT[:, c, :], in_=pst[:, 0:512], func=AF.Copy,
                                                                 scale=vcol(gname, c)),
                     reads=[("ps", b), "vecs"], writes=[(dkey, c)])

        def phaseA_gen(b_, blk_, xs0, xn, junk):
            t0_ = blk_ * BT
            P.dma("pool", xs0[:], x_d[b_, t0_:t0_ + BT, :].rearrange("(t p) d -> p t d", p=128), "x0",
                  writes=[("xs0", tt) for tt in range(4)])
            yield
            for tt in range(4):
                P.op("act", lambda e, tt=tt: e.activation(out=junk[:], in_=xs0[:, tt, :], func=AF.Square,
                                                          accum_out=stat[:, tt:tt + 1]),
                     reads=[("xs0", tt)], writes=["junkA", ("stat", 0)])
            rstd_from_ss(0, 1e-6)
            yield
            for tt in range(4):
                P.op("dve", lambda e, tt=tt: e.tensor_scalar(out=xn[:, tt, :], in0=xs0[:, tt, :],
                                                             scalar1=stat[:, tt:tt + 1], scalar2=None, op0=ALU.mult),
                     reads=[("xs0", tt), ("stat", 0)], writes=[("xn", tt)])
                if tt % 2 == 1:
                    yield
            for c in range(8):
                b = bank()
                pst = ps[b][:].bitcast(BF)
                for tt in range(4):
                    P.op("pe", lambda e, tt=tt, c=c, pst=pst: e.transpose(out=pst[:, tt * 128:(tt + 1) * 128],
                                                                          in_=xn[:, tt, c * 128:(c + 1) * 128],
                                                                          identity=ident_bf),
                         reads=[("xn", tt), "cbf"], writes=[("ps", b)])
                P.op("act", lambda e, c=c, pst=pst: e.activation(out=hT[:, c, :], in_=pst[:, 0:512], func=AF.Copy,
                                                                 scale=vcol("g_pre", c)),
                     reads=[("ps", b), "vecs"], writes=[("hT", c)])
                yield

        def post_norm_residual(src, col0, rowoff, junk, tmp, X):
            for tt in range(4):
                P.op("act", lambda e, tt=tt: e.activation(out=junk[:], in_=src[:, tt, :], func=AF.Square,
                                                          accum_out=stat[:, col0 + tt:col0 + tt + 1]),
                     reads=[("mo", tt)], writes=["junk", ("stat", col0)])
            rstd_from_ss(col0, 1e-6)
            for tt in range(4):
                P.op("dve", lambda e, tt=tt: e.scalar_tensor_tensor(out=tmp[:], in0=src[:, tt, :],
                                                                    scalar=stat[:, col0 + tt:col0 + tt + 1],
                                                                    in1=rows[:, rowoff:rowoff + 1024],
                                                                    op0=ALU.mult, op1=ALU.mult),
                     reads=[("mo", tt), ("stat", col0), "rows"], writes=["pn_tmp"])
                P.op("dve", lambda e, tt=tt: e.tensor_tensor(out=X[:, tt, :], in0=X[:, tt, :], in1=tmp[:], op=ALU.add),
                     reads=["pn_tmp", ("X", tt)], writes=[("X", tt)])

        for b_ in range(nseq):
            for blk in range(nblk):
                t0 = blk * BT
                nkt = 4 * blk + 4
                ph = contextlib.ExitStack()
                with ph:
                    def pb(name, shape, dtype):
                        return sbx(ph, name, shape, dtype)
                    pm = pb("pm", [128, 14, 512], BF)
                    P.stage = 1
                    if b_ == 0 and blk == 0:
                        sa = contextlib.ExitStack()
                        with sa:
                            xs0 = sbx(sa, "xs0", [128, 4, 1024], F32)
                            xn = sbx(sa, "xn", [128, 4, 1024], BF)
                            junk = sbx(sa, "junk", [128, 1024], BF)
                            for _ in phaseA_gen(b_, blk, xs0, xn, junk):
                                pass
                            P.fence()
                    sc = contextlib.ExitStack()
                    sc.__enter__()
                    qT = sbx(sc, "qT", [128, 4, 512], BF)
                    pmt = [sbx(sc, f"pmt{i}", [128, 512], F32) for i in range(2)]
                    Ebuf = [sbx(sc, f"E{i}", [128, 2, 512], F32) for i in range(2)]
                    SPb = [sbx(sc, f"SP{i}", [128, 2, 512], BF) for i in range(2)]
                    Wtb = [sbx(sc, f"Wt{i}", [128, 2, 512], BF) for i in range(2)]
                    lsum = [[[sbx(sc, f"lsum{i}_{q}_{j}", [128, 512], BF) for j in range(2)] for q in range(2)] for i in range(2)]
                    hreads = [("hT", c) for c in range(8)]
                    P.stage = 2
                    sl = load_slab(0)
                    for j in range(4):
                        b = bank()
                        P.mm_group([mm(ps[b][:], slab[sl][:, kc * 512 + j * 128: kc * 512 + (j + 1) * 128], hT[:, kc, :],
                                       kc == 0, kc == 7) for kc in range(8)],
                                   reads=[("slab", sl)] + hreads, writes=[("ps", b)])
                        P.op("act", lambda e, j=j, b=b: e.activation(out=qT[:, j, :], in_=ps[b][:], func=AF.Copy, scale=0.125),
                             reads=[("ps", b)], writes=[("qT", j)])
                    sl = load_slab(1)
                    for j in range(4):
                        b = bank()
                        P.mm_group([mm(ps[b][:], slab[sl][:, kc * 512 + j * 128: kc * 512 + (j + 1) * 128], hT[:, kc, :],
                                       kc == 0, kc == 7) for kc in range(8)],
                                   reads=[("slab", sl)] + hreads, writes=[("ps", b)])
                        P.op("dve", lambda e, j=j, b=b, t0=t0: e.tensor_copy(out=kT[:, j, t0:t0 + BT], in_=ps[b][:]),
                             reads=[("ps", b)], writes=[("kT", j)])
                    sl = load_slab(2)
                    for tt in range(4):
                        b = bank()
                        P.mm_group([mm(ps[b][:], hT[:, kc, tt * 128:(tt + 1) * 128], slab[sl][:, kc * 512:(kc + 1) * 512],
                                       kc == 0, kc == 7) for kc in range(8)],
                                   reads=[("slab", sl)] + hreads, writes=[("ps", b)])
                        P.op("act", lambda e, tt=tt, b=b, blk=blk: e.activation(out=vtm[:, blk * 4 + tt, :], in_=ps[b][:], func=AF.Copy),
                             reads=[("ps", b)], writes=[("vtm", blk * 4 + tt)])
                    if blk == 0:
                        P.op("dve", lambda e: e.memset(carry[:], 0.0), writes=["carry"])
                        P.op("dve", lambda e: e.memset(S32[:], 0.0), writes=["S32"])
                    sls = {}
                    for c in range(14):
                        if c % 4 == 0:
                            sls[c // 4] = load_slab(3 + c // 4)
                        sl = sls[c // 4]
                        j = c % 4
                        b = bank()
                        P.mm_group([mm(ps[b][:], slab[sl][:, kc * 512 + j * 128: kc * 512 + (j + 1) * 128], hT[:, kc, :],
                                       kc == 0, kc == 7) for kc in range(8)],
                                   reads=[("slab", sl)] + hreads, writes=[("ps", b)])
                        tm_ = pmt[c % 2]
                        P.op("act", lambda e, c=c, b=b, tm_=tm_: e.activation(out=tm_[:], in_=ps[b][:], func=AF.Copy,
                                                                              scale=onem[:, c:c + 1]),
                             reads=[("ps", b), "onem"], writes=[("pmt", c % 2)])
                        P.op("dve", lambda e, c=c, b=b, tm_=tm_: e.scalar_tensor_tensor(out=pm[:, c, 1:512], in0=ps[b][:, 0:511],
                                                                                        scalar=vcol("mu", c), in1=tm_[:, 1:512],
                                                                                        op0=ALU.mult, op1=ALU.add),
                             reads=[("ps", b), ("pmt", c % 2), "vecs"], writes=[("pm", c)])
                        P.op("dve", lambda e, c=c, tm_=tm_: e.scalar_tensor_tensor(out=pm[:, c, 0:1], in0=carry[:, c:c + 1],
                                                                                   scalar=vcol("mu", c), in1=tm_[:, 0:1],
                                                                                   op0=ALU.mult, op1=ALU.add),
                             reads=["carry", ("pmt", c % 2), "vecs"], writes=[("pm", c)])
                        P.op("dve", lambda e, c=c, b=b: e.tensor_copy(out=carry[:, c:c + 1], in_=ps[b][:, 511:512]),
                             reads=[("ps", b)], writes=["carry"])

                    P.stage = 3
                    trimask = amask(0)[:, 0:128]
                    units = []
                    for hp in range(4):
                        for kb in range(nkt - 1, -1, -1):
                            for hh in range(2):
                                units.append(dict(hp=hp, hh=hh, kb=kb, i=kb - 4 * blk, first=(kb == nkt - 1), last=(kb == 0),
                                                  n=len(units)))
                    par = {0: 0, 1: 0}

                    def stA(u):
                        hp, hh, kb, n = u["hp"], u["hh"], u["kb"], u["n"]
                        base = hh * 64
                        c0_ = max(u["i"], 0) * 128
                        cols = slice(c0_, 512)
                        u["cols"] = cols
                        bz = 2 * ((n // 2) % 2) + hh
                        u["kap"] = kT[base:base + 64, hp, kb * 128:(kb + 1) * 128]
                        u["qap"] = qT[base:base + 64, hp, cols]
                        if u["first"]:
                            for li_, l_ in enumerate(lsum[hh][hp % 2]):
                                P.op("pool", lambda e, l_=l_: e.memset(l_[:], 0.0), writes=[("lsum", hh, hp % 2), ("lsum", hh, hp % 2, li_)])
                            bo = 6 + hp % 2
                            P.op("pe", mm(ps[bo][base:base + 64, :], zeros_bf[:, 0:64], amask(1), True, False),
                                 reads=["zeros", "cbf"], writes=[("ps", bo)])
                        P.mm_group([mm(ps[bz][:, cols], u["kap"], u["qap"])], reads=[("kT", hp), ("qT", hp)], writes=[("ps", bz)])

                    def stA2(ua):
                        p_ = ua["n"] // 2
                        cols = ua["cols"]
                        zb = 2 * (p_ % 2)
                        Ep, SPp = Ebuf[p_ % 2], SPb[p_ % 2]
                        P.op("act", lambda e, Ep=Ep, zb=zb, cols=cols: e.activation(out=Ep[:, :, cols], in_=psall[:, zb:zb + 2, cols], func=AF.Exp),
                             reads=[("ps", zb), ("ps", zb + 1)], writes=[("E", p_ % 2)])
                        P.op("act", lambda e, Ep=Ep, SPp=SPp, cols=cols: e.activation(out=SPp[:, :, cols], in_=Ep[:, :, cols], func=AF.Ln, bias=1.0),
                             reads=[("E", p_ % 2)], writes=[("SP", p_ % 2)])
                        if ua["i"] >= 0:
                            dsl = slice(cols.start, cols.start + 128)
                            P.op("pool", lambda e, SPp=SPp, dsl=dsl: e.tensor_tensor(
                                out=SPp[:, :, dsl], in0=SPp[:, :, dsl],
                                in1=trimask.rearrange("p (a t) -> p a t", a=1).to_broadcast([128, 2, 128]), op=ALU.mult),
                                 reads=[("SP", p_ % 2), "cbf"], writes=[("SP", p_ % 2)])

                    def stB(u):
                        hp, hh, kb, n = u["hp"], u["hh"], u["kb"], u["n"]
                        cols = u["cols"]
                        SPt = SPb[(n // 2) % 2][:, hh, :]
                        Wt = Wtb[(n // 2) % 2][:, hh, :]
                        bw = 4 + hh
                        cur = lsum[hh][hp % 2][par[hh]]
                        nxt = lsum[hh][hp % 2][1 - par[hh]]
                        grp = [mm(ps[bw][:, cols], u["kap"], u["qap"], True, False),
                               mm(ps[bw][:, cols], trineg_bf, SPt[:, cols], False, u["first"])]
                        rd = [("kT", hp), ("qT", hp), ("SP", (n // 2) % 2), "cbf"]
                        if not u["first"]:
                            grp.append(mm(ps[bw][:, cols], onesneg_bf, cur[:, cols], False, True))
                            rd.append(("lsum", hh, hp % 2, par[hh]))
                        u["grpB"] = (grp, rd, bw)

                    def stB2(u):
                        hp, hh, kb, n = u["hp"], u["hh"], u["kb"], u["n"]
                        cols = u["cols"]
                        SPt = SPb[(n // 2) % 2][:, hh, :]
                        Wt = Wtb[(n // 2) % 2][:, hh, :]
                        bw = 4 + hh
                        cur = lsum[hh][hp % 2][par[hh]]
                        nxt = lsum[hh][hp % 2][1 - par[hh]]
                        if not u["last"]:
                            P.op("dve", lambda e, cur=cur, nxt=nxt, SPt=SPt, cols=cols: e.tensor_tensor(out=nxt[:, cols], in0=cur[:, cols], in1=SPt[:, cols], op=ALU.add),
                                 reads=[("SP", (n // 2) % 2), ("lsum", hh, hp % 2, par[hh]), ("lsum", hh, hp % 2)], writes=[("lsum", hh, hp % 2, 1 - par[hh])])
                            par[hh] = 1 - par[hh]
                        else:
                            par[hh] = 0

                    def stB3(ua):
                        p_ = ua["n"] // 2
                        cols = ua["cols"]
                        Wp = Wtb[p_ % 2]
                        P.op("act", lambda e, Wp=Wp, cols=cols: e.activation(out=Wp[:, :, cols], in_=psall[:, 4:6, cols], func=AF.Exp),
                             reads=[("ps", 4), ("ps", 5)], writes=[("Wt", p_ % 2)])
                        if ua["i"] >= 0:
                            dsl = slice(cols.start, cols.start + 128)
                            P.op("pool", lambda e, Wp=Wp, dsl=dsl: e.tensor_tensor(
                                out=Wp[:, :, dsl], in0=Wp[:, :, dsl],
                                in1=trimask.rearrange("p (a t) -> p a t", a=1).to_broadcast([128, 2, 128]), op=ALU.mult),
                                 reads=[("Wt", p_ % 2), "cbf"], writes=[("Wt", p_ % 2)])

                    def stC(u):
                        hp, hh, kb, n = u["hp"], u["hh"], u["kb"], u["n"]
                        base = hh * 64
                        cols = u["cols"]
                        Wt = Wtb[(n // 2) % 2][:, hh, :]
                        bo = 6 + hp % 2
                        h = hp * 2 + hh
                        P.op("pe", mm(ps[bo][base:base + 64, cols], vtm[:, kb, h * 64:(h + 1) * 64], Wt[:, cols], False, u["last"]),
                             reads=[("Wt", (n // 2) % 2), ("vtm", kb)], writes=[("ps", bo)])
                        if u["last"] and hh == 1:
                            P.op("act", lambda e, hp=hp, bo=bo: e.activation(out=osbT[:, hp, :], in_=ps[bo][:], func=AF.Copy),
                                 reads=[("ps", bo)], writes=[("osbT", hp)])

                    NU = len(units)
                    NP = NU // 2
                    for it in range(NP + 2):
                        if it < NP:
                            stA(units[2 * it])
                            stA(units[2 * it + 1])
                            stA2(units[2 * it])
                        if 0 <= it - 2 < NP and PAIRMODE == 1:
                            stC(units[2 * (it - 2)])
                        if 0 <= it - 1 < NP:
                            ua, ub = units[2 * (it - 1)], units[2 * (it - 1) + 1]
                            stB(ua)
                            stB(ub)
                            (ga, ra, bwa), (gb, rb_, bwb) = ua["grpB"], ub["grpB"]
                            P.mm_group([ga[0], gb[0]] + ga[1:] + gb[1:], reads=ra + rb_, writes=[("ps", bwa), ("ps", bwb)])
                            stB2(ua)
                            stB2(ub)
                            stB3(ua)
                        if 0 <= it - 2 < NP:
                            if PAIRMODE != 1:
                                stC(units[2 * (it - 2)])
                            stC(units[2 * (it - 2) + 1])
                    P.fence(dmas=("y",))
                    sc.__exit__(None, None, None)
                    P.stage = 4
                    rw = contextlib.ExitStack()
                    with rw:
                        def rb(name, shape, dtype):
                            return sbx(rw, name, shape, dtype)
                        lin = rb("lin", [128, 512], BF)
                        sgx = rb("sgx", [128, 512], BF)
                        sigw = rb("sigw", [128, 512], F32)
                        aa = rb("aa", [128, 512], F32)
                        sq = rb("sq", [128, 512], BF)
                        kk = rb("kk", [128, 512], F32)
                        kmod = rb("kmod", [128, 512], F32)
                        bb = rb("bb", [128, 512], F32)
                        Lp = rb("Lp", [128, 512], F32)
                        Dd = rb("Dd", [128, 512], F32)
                        Dd2 = rb("Dd2", [128, 512], F32)
                        e1 = rb("e1", [128, 512], F32)
                        e2 = rb("e2", [128, 512], F32)
                        e3 = rb("e3", [128, 512], F32)
                        gamC = rb("gamC", [128, 8], F32)
                        AR = rb("AR", [128, 4, 2, 128], BF)
                        BK = rb("BK", [128, 4, 2, 128], BF)
                        vbf = rb("vbf", [128, 512], BF)
                        rkr = rb("rkr", [128, 512], BF)
                        TM4 = rb("TM4", [128, 4, 4, 128], BF)
                        TMb4 = rb("TMb4", [128, 4, 2, 2, 128], BF)
                        MMs4g = [rb(f"MMs4_{g}", [128, 4, 512], BF) for g in range(2)]
                        PP4g = [[rb(f"PP4_{g}_{l}", [128, 4, 256], BF) for l in range(5)] for g in range(2)]
                        PT54g = [rb(f"PT54_{g}", [128, 4, 128], BF) for g in range(2)]
                        Z4g = [[rb(f"Z4_{g}_{i}", [128, 4, 128], BF) for i in range(2)] for g in range(2)]
                        U0b4g = [rb(f"U0b4_{g}", [128, 4, 2, 64], BF) for g in range(2)]
                        RhT = rb("RhT", [128, 512], BF)
                        QT = rb("QT", [128, 8, 128], BF)
                        Sbr = [rb(f"Sbr{i}", [128, 64], F32) for i in range(2)]
                        Xs = rb("Xs", [128, 64], F32)
                        gsh = rb("gsh", [128, 8], F32)
                        Sbf = rb("Sbf", [128, 64], BF)
                        Sbd = rb("Sbd", [128, 128], BF)
                        gamCs = [gamC, rb("gamC1", [128, 8], F32), rb("gamC2", [128, 8], F32)]
                        rkrs = [rkr, rb("rkr1", [128, 512], BF), rb("rkr2", [128, 512], BF)]
                        Y0T = rb("Y0T", [128, 512], F32)
                        Hs = rb("Hs", [128, 512], F32)
                        YT = rb("YT", [128, 512], F32)
                        cenP = rb("cenP", [128, 512], F32)
                        tmpP = rb("tmpP", [128, 512], F32)
                        t1, rn = Dd2, e3
                        cen, sqc, rs, bonus, gT = cenP, tmpP, tmpP, tmpP, tmpP

                        P.op("pool", lambda e: e.memset(QT[:], 0.0), writes=["QT"])
                        P.op("pool", lambda e: e.memset(Sbd[:], 0.0), writes=["Sbd"])
                        P.op("act", lambda e: e.activation(out=lin[0:64, :], in_=pm[0:64, 12, :], func=AF.Tanh),
                             reads=[("pm", 12)], writes=["lin0"])
                        P.op("act", lambda e: e.activation(out=lin[64:128, :], in_=pm[64:128, 12, :], func=AF.Copy),
                             reads=[("pm", 12)], writes=["lin1"])
                        P.op("act", lambda e: e.activation(out=sgx[:], in_=pm[:, 13, :], func=AF.Sigmoid),
                             reads=[("pm", 13)], writes=["sgx"])
                        ucnt = [0]
                        def prep_gen(fc):
                            P.stage = 4
                            R = pm[:, fc, :]
                            Kr = pm[:, 4 + fc, :]
                            V = pm[:, 8 + fc, :]
                            rk = [("pm", fc), ("pm", 4 + fc), ("pm", 8 + fc)]
                            fsl = slice(fc * 128, (fc + 1) * 128)
                            b = bank(0, 4)
                            P.mm_group([mm(ps[b][:], lwa_bf[0:64, fsl], lin[0:64, :])], reads=["lwa", "lin0"], writes=[("ps", b)])
                            P.op("act", lambda e, b=b, fc=fc: e.activation(out=sigw[:], in_=ps[b][:], func=AF.Sigmoid, bias=vcol("w0", fc)),
                                 reads=[("ps", b), "vecs"], writes=["sigw"])
                            b = bank(0, 4)
                            P.mm_group([mm(ps[b][:], lwa_bf[64:128, fsl], lin[64:128, :])], reads=["lwa", "lin1"], writes=[("ps", b)])
                            P.op("act", lambda e, b=b, fc=fc: e.activation(out=aa[:], in_=ps[b][:], func=AF.Sigmoid, bias=vcol("a0", fc)),
                                 reads=[("ps", b), "vecs"], writes=["aa"])
                            yield
                            P.op("act", lambda e, fc=fc, Kr=Kr: e.activation(out=sq[:], in_=Kr, func=AF.Square, scale=vcol("k_k", fc)),
                                 reads=rk + ["vecs"], writes=["sq"])
                            b = bank(0, 4)
                            P.mm_group([mm(ps[b][:], bones_bf, sq[:])], reads=["cbf", "sq"], writes=[("ps", b)])
                            P.op("act", lambda e, b=b: e.activation(out=rn[:], in_=ps[b][:], func=AF.Ln, bias=1e-24),
                                 reads=[("ps", b)], writes=["e3"])
                            P.op("act", lambda e: e.activation(out=rn[:], in_=rn[:], func=AF.Exp, scale=-0.5),
                                 reads=["e3"], writes=["e3"])
                            P.op("dve", lambda e, fc=fc, Kr=Kr: e.scalar_tensor_tensor(out=kk[:], in0=Kr, scalar=vcol("k_k", fc), in1=rn[:],
                                                                                       op0=ALU.mult, op1=ALU.mult),
                                 reads=rk + ["e3", "vecs"], writes=["kk"])
                            yield
                            P.op("dve", lambda e, fc=fc: e.tensor_scalar(out=t1[:], in0=aa[:], scalar1=vcol("k_a", fc),
                                                                         scalar2=onem[:, 16 + fc:17 + fc], op0=ALU.mult, op1=ALU.add),
                                 reads=["aa", "vecs", "onem"], writes=["Dd2"])
                            P.op("dve", lambda e, Kr=Kr: e.tensor_tensor(out=kmod[:], in0=Kr, in1=t1[:], op=ALU.mult),
                                 reads=rk + ["Dd2"], writes=["kmod"])
                            P.op("pool", lambda e: e.tensor_tensor(out=bb[:], in0=kk[:], in1=aa[:], op=ALU.mult),
                                 reads=["kk", "aa"], writes=["bb"])
                            yield
                            P.stage = 4.1
                            P.op("dve", lambda e: e.tensor_tensor_scan(out=Lp[:], data0=cmask, data1=sigw[:], initial=0.0,
                                                                       op0=ALU.mult, op1=ALU.add),
                                 reads=["cbf", "sigw"], writes=["Lp"])
                            Lp3 = Lp[:].rearrange("p (c t) -> p c t", t=64)
                            P.op("dve", lambda e, Lp3=Lp3: e.tensor_tensor(out=Dd[:].rearrange("p (c t) -> p c t", t=64),
                                                                           in0=Lp3[:, :, 63:64].to_broadcast([128, 8, 64]),
                                                                           in1=Lp3, op=ALU.subtract),
                                 reads=["Lp"], writes=["Dd"])
                            P.op("pool", lambda e: e.tensor_tensor(out=Dd2[:], in0=Dd[:], in1=sigw[:], op=ALU.add),
                                 reads=["Dd", "sigw"], writes=["Dd2"])
                            yield
                            P.op("act", lambda e: e.activation(out=e1[:], in_=Dd[:], func=AF.Exp, scale=C0), reads=["Dd"], writes=["e1"])
                            P.op("act", lambda e: e.activation(out=e2[:], in_=Dd[:], func=AF.Exp, scale=-C0), reads=["Dd"], writes=["e2"])
                            P.op("act", lambda e: e.activation(out=e3[:], in_=Dd2[:], func=AF.Exp, scale=C0), reads=["Dd2"], writes=["e3"])
                            yield
                            P.op("act", lambda e, Lp3=Lp3: e.activation(out=gamCs[fc % 3][:].rearrange("p (c o) -> p c o", o=1), in_=Lp3[:, :, 63:64],
                                                                        func=AF.Exp, scale=-C0),
                                 reads=["Lp"], writes=[("gamC", fc % 3)])
                            P.op("act", lambda e, V=V: e.activation(out=vbf[:], in_=V, func=AF.Copy), reads=rk, writes=["vbf"])
                            P.op("dve", lambda e, fc=fc, R=R: e.scalar_tensor_tensor(out=rkrs[fc % 3][:], in0=R, scalar=vcol("r_k", fc), in1=kmod[:],
                                                                                     op0=ALU.mult, op1=ALU.mult),
                                 reads=rk + ["kmod", "vecs"], writes=[("rkr", fc % 3)])
                            yield "SPLIT"
                            P.stage = 4.2
                            v4 = lambda t: t[:].rearrange("p (c t) -> p c t", t=128)
                            P.op("dve", lambda e: e.scalar_tensor_tensor(out=AR[:, :, 0, :], in0=v4(kk), scalar=-1.0, in1=v4(e3),
                                                                         op0=ALU.mult, op1=ALU.mult),
                                 reads=["kk", "e3"], writes=["AR"])
                            P.op("pool", lambda e, R=R: e.tensor_tensor(out=AR[:, :, 1, :], in0=R.rearrange("p (c t) -> p c t", t=128),
                                                                        in1=v4(e1), op=ALU.mult),
                                 reads=rk + ["e1"], writes=["AR"])
                            yield
                            P.op("dve", lambda e: e.tensor_tensor(out=BK[:, :, 0, :], in0=v4(bb), in1=v4(e2), op=ALU.mult),
                                 reads=["bb", "e2"], writes=["BK"])
                            P.op("pool", lambda e: e.tensor_tensor(out=BK[:, :, 1, :], in0=v4(kmod), in1=v4(e2), op=ALU.mult),
                                 reads=["kmod", "e2"], writes=["BK"])
                            yield
                            yield
                            yield "SPLIT2"
                            P.stage = 4.3
                            for cp2 in range(2):
                                b = bank(0, 4)
                                pst = ps[b][:].bitcast(BF)
                                for ci in range(2):
                                    cp = cp2 * 2 + ci
                                    srcs = [AR[:, cp, 0, :], BK[:, cp, 0, :], BK[:, cp, 1, :], vbf[:, cp * 128:(cp + 1) * 128]]
                                    for qi, sap in enumerate(srcs):
                                        o_ = ci * 512 + qi * 128
                                        P.op("pe", lambda e, o_=o_, sap=sap, pst=pst: e.transpose(out=pst[:, o_:o_ + 128], in_=sap, identity=ident_bf),
                                             reads=["AR", "BK", "vbf", "cbf"], writes=[("ps", b)])
                                P.op("act", lambda e, cp2=cp2, pst=pst: e.activation(
                                    out=TM4[:, cp2 * 2:cp2 * 2 + 2].rearrange("p c a b -> p (c a b)"), in_=pst[:, 0:1024], func=AF.Copy),
                                     reads=[("ps", b)], writes=["TM4"])
                            for c2 in range(2):
                                for qd, qs in ((0, 1), (1, 3)):
                                    P.op("dve", lambda e, c2=c2, qd=qd, qs=qs: e.tensor_scalar(
                                        out=TMb4[:, :, c2, qd, :], in0=TM4[:, :, qs, :], scalar1=cm01[:, c2:c2 + 1], scalar2=None, op0=ALU.mult),
                                        reads=["TM4", "cbf"], writes=["TMb4"])
                            yield
                        def groups(fc):
                            hs = None
                            P.stage = 4.4
                            def grp(hh, MMs4, PP4, PT54, Z4, U0b4, gk, rot):
                                hs = slice(hh * 64, hh * 64 + 64)
                                for cp in range(4):
                                    arf = AR[hs, cp].rearrange("p a t -> p (a t)")
                                    P.mm_group([mm(ps[cp][:, 0:256], BK[hs, cp, 0, :], arf)], reads=["AR", "BK"], writes=[("ps", cp)])
                                    P.mm_group([mm(ps[cp][:, 256:512], BK[hs, cp, 1, :], arf)], reads=["AR", "BK"], writes=[("ps", cp)])
                                    P.op("dve", lambda e, cp=cp: e.tensor_tensor(out=MMs4[:, cp, :], in0=ps[cp][:], in1=mask4, op=ALU.mult),
                                         reads=[("ps", cp), "cbf"], writes=[("MMs4", gk)])
                                yield
                                for cp in range(4):
                                    P.mm_group([mm(ps[0][:, cp * 128:(cp + 1) * 128], AR[hs, cp, 0, :], BK[hs, cp, 0, :])],
                                               reads=["AR", "BK"], writes=[("ps", 0)])
                                P.op("dve", lambda e: e.tensor_tensor(out=PP4[0][:, :, 0:128], in0=ps[0][:].rearrange("p (c t) -> p c t", c=4),
                                                                      in1=masklt.rearrange("p (a t) -> p a t", a=1).to_broadcast([128, 4, 128]), op=ALU.mult),
                                     reads=[("ps", 0), "cbf"], writes=[("PP4", gk, 0)])
                                for cp in range(4):
                                    P.mm_group([mm(ps[2][:, cp * 64:(cp + 1) * 64], MMs4[:, cp, 256:384], TM4[:, cp, 3, hs])],
                                               reads=[("MMs4", gk), "TM4"], writes=[("ps", 2)])
                                P.op("act", lambda e, hs=hs: e.activation(out=Z4[0][:, :, 0:64], in_=TM4[:, :, 0, hs], func=AF.Copy),
                                     reads=["TM4"], writes=[("Z4", gk, 0)])
                                P.op("act", lambda e: e.activation(out=Z4[0][:, :, 64:128], in_=ps[2][:, 0:256].rearrange("p (c t) -> p c t", c=4), func=AF.Copy),
                                     reads=[("ps", 2)], writes=[("Z4", gk, 0)])
                                yield
                                P.stage = 4.5
                                bsel = [0]

                                def nb():
                                    b_ = (1, 3, 0, 2)[(bsel[0] + rot) % 4]
                                    bsel[0] += 1
                                    return b_
                                for l in range(6):
                                    zi, zo = Z4[l % 2], Z4[(l + 1) % 2]
                                    pair = None
                                    if l < 4:
                                        pair = (nb(), nb())
                                        for cp in range(4):
                                            bk = pair[cp // 2]
                                            off = (cp % 2) * 256
                                            Pl = PP4[l][:, cp, 0:128]
                                            PTl = PP4[l][:, cp, 128:256] if l > 0 else MMs4[:, cp, 0:128]
                                            P.mm_group([mm(ps[bk][:, off:off + 128], PTl, Pl)], reads=[("PP4", gk, l), ("MMs4", gk)], writes=[("ps", bk)])
                                            P.mm_group([mm(ps[bk][:, off + 128:off + 256], Pl, PTl)], reads=[("PP4", gk, l), ("MMs4", gk)], writes=[("ps", bk)])
                                    elif l == 4:
                                        b5 = nb()
                                        for cp in range(4):
                                            P.mm_group([mm(ps[b5][:, cp * 128:(cp + 1) * 128], PP4[4][:, cp, 0:128], PP4[4][:, cp, 128:256])],
                                                       reads=[("PP4", gk, 4)], writes=[("ps", b5)])
                                    ba = nb()
                                    for cp in range(4):
                                        PTl = (PP4[l][:, cp, 128:256] if l > 0 else MMs4[:, cp, 0:128]) if l < 5 else PT54[:, cp, :]
                                        P.mm_group([mm(ps[ba][:, cp * 128:(cp + 1) * 128], PTl, zi[:, cp, :])],
                                                   reads=[("PP4", gk, l) if l < 5 else ("PT54", gk), ("MMs4", gk), ("Z4", gk, l % 2)], writes=[("ps", ba)])
                                    if l < 4:
                                        for half in range(2):
                                            bk = pair[half]
                                            if half == 0 or gk == 1:
                                                P.op("act", lambda e, l=l, half=half, bk=bk: e.activation(
                                                    out=PP4[l + 1][:, 2 * half:2 * half + 2, :].rearrange("p c t -> p (c t)"), in_=ps[bk][:], func=AF.Copy),
                                                    reads=[("ps", bk)], writes=[("PP4", gk, l + 1)])
                                            else:
                                                P.op("dve", lambda e, l=l, half=half, bk=bk: e.tensor_copy(
                                                    out=PP4[l + 1][:, 2 * half:2 * half + 2, :].rearrange("p c t -> p (c t)"), in_=ps[bk][:]),
                                                    reads=[("ps", bk)], writes=[("PP4", gk, l + 1)])
                                    elif l == 4:
                                        P.op("act", lambda e, b5=b5: e.activation(out=PT54[:].rearrange("p c t -> p (c t)"), in_=ps[b5][:], func=AF.Copy),
                                             reads=[("ps", b5)], writes=[("PT54", gk)])
                                    P.op("dve", lambda e, zi=zi, zo=zo, ba=ba: e.tensor_tensor(
                                        out=zo[:].rearrange("p c t -> p (c t)"), in0=ps[ba][:], in1=zi[:].rearrange("p c t -> p (c t)"), op=ALU.add),
                                        reads=[("ps", ba), ("Z4", gk, l % 2)], writes=[("Z4", gk, (l + 1) % 2)])
                                    yield
                                P.stage = 4.6
                                Zf = Z4[0]
                                for c2 in range(2):
                                    P.op("dve", lambda e, Zf=Zf, c2=c2: e.tensor_scalar(
                                        out=U0b4[:, :, c2, :], in0=Zf[:, :, 64:128], scalar1=cm01[:, c2:c2 + 1], scalar2=None, op0=ALU.mult),
                                        reads=[("Z4", gk, 0), "cbf"], writes=[("U0b4", gk)])
                                zk = [("Z4", gk, 0), ("MMs4", gk), "TM4", "TMb4", ("U0b4", gk)]
                                for cp in range(4):
                                    csl = slice(cp * 128, (cp + 1) * 128)
                                    MrbT, MrkT = MMs4[:, cp, 128:256], MMs4[:, cp, 384:512]
                                    Vtm = TM4[:, cp, 3, hs]
                                    P.mm_group([mm(ps[4][hs, csl], Zf[:, cp, 64:128], MrbT, True, False),
                                                mm(ps[4][hs, csl], Vtm, MrkT, False, True)], reads=zk, writes=[("ps", 4)])
                                    P.mm_group([mm(ps[5][hs, csl], Zf[:, cp, 0:64], MrbT)], reads=zk, writes=[("ps", 5)])
                                    o6 = ps[6][hs, csl].rearrange("p (a b) -> p a b", a=2)
                                    o7 = ps[7][hs, csl].rearrange("p (a b) -> p a b", a=2)
                                    P.mm_group([mm(o6, Zf[:, cp, 0:64], TMb4[:, cp, :, 0, hs])], reads=zk, writes=[("ps", 6)])
                                    P.mm_group([mm(o7, TM4[:, cp, 1, hs], U0b4[:, cp], True, False),
                                                mm(o7, TM4[:, cp, 2, hs], TMb4[:, cp, :, 1, hs], False, True)],
                                               reads=zk, writes=[("ps", 7)])
                            gens = [grp(0, MMs4g[0], PP4g[0], PT54g[0], Z4g[0], U0b4g[0], 0, 0), grp(1, MMs4g[1], PP4g[1], PT54g[1], Z4g[1], U0b4g[1], 1, 2)]
                            live = list(gens)
                            while live:
                                for g_ in list(live):
                                    try:
                                        next(g_)
                                    except StopIteration:
                                        live.remove(g_)
                                yield
                            P.stage = 4.7
                            P.op("act", lambda e: e.activation(out=Y0T[:], in_=ps[4][:], func=AF.Copy), reads=[("ps", 4)], writes=["Y0T"])
                            P.op("dve", lambda e: e.tensor_tensor(out=RhT[:].rearrange("p (c t) -> p c t", t=128),
                                                                  in0=ps[5][:].rearrange("p (c t) -> p c t", t=128),
                                                                  in1=AR[:, :, 1, :], op=ALU.add),
                                 reads=[("ps", 5), "AR"], writes=["RhT"])
                            for hh in range(2):
                                hs = slice(hh * 64, hh * 64 + 64)
                                P.op("act", lambda e, hh=hh, hs=hs: e.activation(out=QT[hs, :, hh * 64:(hh + 1) * 64],
                                                                                 in_=ps[6][hs, :].rearrange("p (c k) -> p c k", k=64), func=AF.Copy),
                                     reads=[("ps", 6)], writes=["QT"])
                            P.op("dve", lambda e: e.tensor_copy(out=Hs[:], in_=ps[7][:]), reads=[("ps", 7)], writes=["Hs"])
                        def tail_gen(fc):
                            R = pm[:, fc, :]
                            Kr = pm[:, 4 + fc, :]
                            V = pm[:, 8 + fc, :]
                            rk = [("pm", fc), ("pm", 4 + fc), ("pm", 8 + fc)]
                            fsl = slice(fc * 128, (fc + 1) * 128)
                            P.stage = 4.8
                            P.op("dve", lambda e: e.tensor_copy(out=gsh[:, 0:7], in_=gamCs[fc % 3][:, 1:8]), reads=[("gamC", fc % 3)], writes=["gsh"])
                            P.op("dve", lambda e: e.memset(gsh[:, 7:8], 1.0), writes=["gsh"])
                            P.op("dve", lambda e: e.tensor_tensor(out=Hs[:].rearrange("p (c v) -> p c v", v=64),
                                                                  in0=Hs[:].rearrange("p (c v) -> p c v", v=64),
                                                                  in1=gsh[:].rearrange("p (c o) -> p c o", o=1).to_broadcast([128, 8, 64]), op=ALU.mult),
                                 reads=["Hs", "gsh"], writes=["Hs"])
                            P.op("dve", lambda e, fc=fc: e.tensor_scalar(out=Sbr[0][:], in0=S32[:, fc, :], scalar1=gamCs[fc % 3][:, 0:1],
                                                                         scalar2=None, op0=ALU.mult),
                                 reads=["S32", ("gamC", fc % 3)], writes=[("Sbr", 0)])
                            for c in range(8):
                                cc = slice(c * 64, c * 64 + 64)
                                sb_c = Sbr[c % 2]
                                P.op("act", lambda e, sb_c=sb_c: e.activation(out=Sbf[:], in_=sb_c[:], func=AF.Copy), reads=[("Sbr", c % 2)], writes=["Sbf"])
                                for hh in range(2):
                                    hs = slice(hh * 64, hh * 64 + 64)
                                    P.op("pool", lambda e, hh=hh, hs=hs, sb_c=sb_c: e.tensor_copy(out=Sbd[hs, hh * 64:(hh + 1) * 64], in_=sb_c[hs, :]),
                                         reads=[("Sbr", c % 2)], writes=["Sbd"])
                                bs = bank(0, 4)
                                P.mm_group([mm(ps[bs][:, 0:64], QT[:, c, :], Sbf[:])], reads=["QT", "Sbf"], writes=[("ps", bs)])
                                P.mm_group([mm(ps[bs][:, 64:128], Sbd[:], RhT[:, cc])], reads=["RhT", "Sbd"], writes=[("ps", bs)])
                                P.op("dve", lambda e, c=c, cc=cc, sb_c=sb_c: e.scalar_tensor_tensor(out=Xs[:], in0=sb_c[:], scalar=gsh[:, c:c + 1], in1=Hs[:, cc],
                                                                                                    op0=ALU.mult, op1=ALU.add),
                                     reads=[("Sbr", c % 2), "gsh", "Hs"], writes=["Xs"])
                                dst = Sbr[(c + 1) % 2][:] if c < 7 else S32[:, fc, :]
                                P.op("dve", lambda e, c=c, bs=bs, dst=dst: e.scalar_tensor_tensor(out=dst, in0=ps[bs][:, 0:64], scalar=gsh[:, c:c + 1], in1=Xs[:],
                                                                                                  op0=ALU.mult, op1=ALU.add),
                                     reads=[("ps", bs), "gsh", "Xs"], writes=[("Sbr", (c + 1) % 2) if c < 7 else "S32"])
                                P.op("dve", lambda e, bs=bs, cc=cc: e.tensor_tensor(out=YT[:, cc], in0=ps[bs][:, 64:128], in1=Y0T[:, cc], op=ALU.add),
                                     reads=[("ps", bs), "Y0T"], writes=["YT"])
                                yield
                            P.stage = 4.9
                            b = bank(0, 4)
                            P.mm_group([mm(ps[b][:], bones_f[:], YT[:])], reads=["bones_f", "YT"], writes=[("ps", b)])
                            P.op("dve", lambda e, b=b: e.scalar_tensor_tensor(out=cen[:], in0=ps[b][:], scalar=-1.0 / 64, in1=YT[:],
                                                                              op0=ALU.mult, op1=ALU.add),
                                 reads=[("ps", b), "YT"], writes=["cenP"])
                            P.op("act", lambda e: e.activation(out=sqc[:], in_=cen[:], func=AF.Square), reads=["cenP"], writes=["tmpP"])
                            yield
                            b = bank(0, 4)
                            P.mm_group([mm(ps[b][:], bones_f[:], sqc[:])], reads=["bones_f", "tmpP"], writes=[("ps", b)])
                            P.op("act", lambda e, b=b: e.activation(out=rs[:], in_=ps[b][:], func=AF.Ln, scale=1.0 / 64, bias=64e-5),
                                 reads=[("ps", b)], writes=["tmpP"])
                            P.op("act", lambda e: e.activation(out=rs[:], in_=rs[:], func=AF.Exp, scale=-0.5),
                                 reads=["tmpP"], writes=["tmpP"])
                            P.op("dve", lambda e: e.tensor_tensor(out=cen[:], in0=cen[:], in1=rs[:], op=ALU.mult),
                                 reads=["cenP", "tmpP"], writes=["cenP"])
                            P.op("dve", lambda e, fc=fc: e.tensor_scalar(out=cen[:], in0=cen[:], scalar1=vcol("lnx_w", fc),
                                                                         scalar2=vcol("lnx_b", fc), op0=ALU.mult, op1=ALU.add),
                                 reads=["cenP", "vecs"], writes=["cenP"])
                            b = bank(0, 4)
                            P.mm_group([mm(ps[b][:], bones_bf, rkrs[fc % 3][:])], reads=["cbf", ("rkr", fc % 3)], writes=[("ps", b)])
                            P.op("dve", lambda e, b=b, V=V: e.tensor_tensor(out=bonus[:], in0=ps[b][:], in1=V, op=ALU.mult),
                                 reads=[("ps", b), ("pm", 8 + fc)], writes=["tmpP"])
                            P.op("dve", lambda e: e.tensor_tensor(out=cen[:], in0=cen[:], in1=bonus[:], op=ALU.add),
                                 reads=["cenP", "tmpP"], writes=["cenP"])
                            b = bank(0, 4)
                            P.mm_group([mm(ps[b][:], lg_bf[:, fsl], sgx[:])], reads=["lg", "sgx"], writes=[("ps", b)])
                            P.op("act", lambda e, b=b: e.activation(out=gT[:], in_=ps[b][:], func=AF.Copy),
                                 reads=[("ps", b)], writes=["tmpP"])
                            P.op("dve", lambda e, fc=fc: e.tensor_tensor(out=orwT[:, fc, :], in0=cen[:], in1=gT[:], op=ALU.mult),
                                 reads=["cenP", "tmpP"], writes=[("orwT", fc)])
                            yield
                        def drive(*gens):
                            live = list(gens)
                            while live:
                                for g_ in list(live):
                                    try:
                                        next(g_)
                                    except StopIteration:
                                        live.remove(g_)

                        def drive3(gmain, gtail, gprep, gpre=None):
                            live = [g_ for g_ in (gpre, gtail, gmain, gprep) if g_ is not None]
                            parked = False
                            while live:
                                for g_ in list(live):
                                    if g_ is gprep and parked:
                                        if len(live) == 1:
                                            return
                                        continue
                                    try:
                                        r_ = next(g_)
                                        if g_ is gprep and r_ == "SPLIT":
                                            parked = True
                                    except StopIteration:
                                        live.remove(g_)

                        def drive_until(g_, tag):
                            while True:
                                try:
                                    if next(g_) == tag:
                                        return
                                except StopIteration:
                                    return
                        gcur = prep_gen(0)
                        drive_until(gcur, "SPLIT2")
                        for fc in range(4):
                            gp = prep_gen(fc + 1) if fc < 3 else None
                            drive3(groups(fc), tail_gen(fc - 1) if fc > 0 else None, gp, gcur)
                            if gp is not None:
                                drive_until(gp, "SPLIT2")
                            gcur = gp
                        drive(tail_gen(3))
                    P.fence()
                P.fence()
                P.stage = 5
                ph = contextlib.ExitStack()
                with ph:
                    def pb(name, shape, dtype):
                        return sbx(ph, name, shape, dtype)
                    actT = pb("actT", [128, 22, 512], BF)
                    mo = pb("mo", [128, 4, 1024], F32)
                    mergedT = pb("mergedT", [128, 8, 512], BF)
                    X = pb("X", [128, 4, 1024], F32)
                    P.dma("pool", X[:], x_d[b_, t0:t0 + BT, :].rearrange("(t p) d -> p t d", p=128), "x",
                          writes=[("X", tt) for tt in range(4)])
                    xn = pb("xn", [128, 4, 1024], BF)
                    junk = pb("junk", [128, 1024], BF)
                    h2T = pb("h2T", [128, 8, 512], BF)
                    g1 = pb("g1", [128, 512], F32)
                    g2 = pb("g2", [128, 512], F32)
                    m1 = pb("m1", [128, 512], F32)
                    sg = m1
                    pnt = pb("pnt", [128, 1024], F32)
                    nxt_blk = (b_, blk + 1) if blk + 1 < nblk else ((b_ + 1, 0) if b_ + 1 < nseq else None)
                    if nxt_blk is not None:
                        xs0n = pb("xs0n", [128, 4, 1024], F32)
                        junkA = pb("junkA", [128, 1024], BF)
                    hreads = [("hT", c) for c in range(8)]
                    gsl = {}
                    sl_sb = sl_rw = None
                    for j in range(8):
                        if j % 4 == 0:
                            gsl[0] = load_slab(7 + j // 4)
                            gsl[1] = load_slab(9 + j // 4)
                            if j == 0:
                                sl_sb = load_slab(11)
                                sl_rw = load_slab(12)
                        jj = j % 4
                        for gi_, gt in ((0, g1), (1, g2)):
                            b = bank()
                            sl = gsl[gi_]
                            P.mm_group([mm(ps[b][:], slab[sl][:, kc * 512 + jj * 128: kc * 512 + (jj + 1) * 128], hT[:, kc, :],
                                           kc == 0, kc == 7) for kc in range(8)],
                                       reads=[("slab", sl)] + hreads, writes=[("ps", b)])
                            P.op("act", lambda e, b=b, gt=gt, col=gi_ * 8 + j: e.activation(out=gt[:], in_=ps[b][:], func=AF.Sigmoid,
                                                                                           bias=vcol("b_gate", col)),
                                 reads=[("ps", b), "vecs"], writes=[("g", gi_)])
                        b = bank()
                        P.mm_group([mm(ps[b][:], slab[sl_sb][:, kc * 1024 + j * 128: kc * 1024 + (j + 1) * 128], osbT[:, kc, :],
                                       kc == 0, kc == 3) for kc in range(4)],
                                   reads=[("slab", sl_sb)] + [("osbT", c) for c in range(4)], writes=[("ps", b)])
                        P.op("dve", lambda e, b=b: e.tensor_tensor(out=m1[:], in0=ps[b][:], in1=g1[:], op=ALU.mult),
                             reads=[("ps", b), ("g", 0)], writes=["m1"])
                        b = bank()
                        P.mm_group([mm(ps[b][:], slab[sl_rw][:, kc * 1024 + j * 128: kc * 1024 + (j + 1) * 128], orwT[:, kc, :],
                                       kc == 0, kc == 3) for kc in range(4)],
                                   reads=[("slab", sl_rw)] + [("orwT", c) for c in range(4)], writes=[("ps", b)])
                        P.op("dve", lambda e, b=b: e.tensor_tensor(out=g2[:], in0=ps[b][:], in1=g2[:], op=ALU.mult),
                             reads=[("ps", b), ("g", 1)], writes=[("g", 1)])
                        P.op("dve", lambda e, j=j: e.tensor_tensor(out=mergedT[:, j, :], in0=m1[:], in1=g2[:], op=ALU.add),
                             reads=["m1", ("g", 1)], writes=[("mergedT", j)])
                    mreads = [("mergedT", c) for c in range(8)]
                    sl_wo = [load_slab(13), load_slab(14)]

                    def wo_tt(tt):
                        P.stage = 5.1
                        for hf in range(2):
                            sl = sl_wo[hf]
                            b = bank()
                            P.mm_group([mm(ps[b][:], mergedT[:, kc, tt * 128:(tt + 1) * 128], slab[sl][:, kc * 512:(kc + 1) * 512],
                                           kc == 0, kc == 7) for kc in range(8)],
                                       reads=[("slab", sl)] + mreads, writes=[("ps", b)])
                            P.op("dve", lambda e, b=b, tt=tt, hf=hf: e.tensor_copy(out=mo[:, tt, hf * 512:(hf + 1) * 512], in_=ps[b][:]),
                                 reads=[("ps", b)], writes=[("mo", tt)])

                    def epi_tt(tt):
                        P.stage = 5.2
                        c1 = slice(4 + tt, 5 + tt)
                        c2 = slice(8 + tt, 9 + tt)
                        P.op("act", lambda e, tt=tt, c1=c1: e.activation(out=junk[:], in_=mo[:, tt, :], func=AF.Square, accum_out=stat[:, c1]),
                             reads=[("mo", tt)], writes=["junk", ("st1", tt)])
                        P.op("act", lambda e, c1=c1: e.activation(out=stat[:, c1], in_=stat[:, c1], func=AF.Ln, scale=1.0 / D, bias=1e-6),
                             reads=[("st1", tt)], writes=[("st1", tt)])
                        P.op("act", lambda e, c1=c1: e.activation(out=stat[:, c1], in_=stat[:, c1], func=AF.Exp, scale=-0.5),
                             reads=[("st1", tt)], writes=[("st1", tt)])
                        P.op("dve", lambda e, tt=tt, c1=c1: e.scalar_tensor_tensor(out=pnt[:], in0=mo[:, tt, :], scalar=stat[:, c1],
                                                                                   in1=rows[:, 0:1024], op0=ALU.mult, op1=ALU.mult),
                             reads=[("mo", tt), ("st1", tt), "rows"], writes=["pn_tmp"])
                        P.op("dve", lambda e, tt=tt: e.tensor_tensor(out=X[:, tt, :], in0=X[:, tt, :], in1=pnt[:], op=ALU.add),
                             reads=["pn_tmp", ("X", tt)], writes=[("X", tt)])
                        P.stage = 5.3
                        P.op("act", lambda e, tt=tt, c2=c2: e.activation(out=junk[:], in_=X[:, tt, :], func=AF.Square, accum_out=stat[:, c2]),
                             reads=[("X", tt)], writes=["junk", ("st2", tt)])
                        P.op("act", lambda e, c2=c2: e.activation(out=stat[:, c2], in_=stat[:, c2], func=AF.Ln, scale=1.0 / D, bias=1e-6),
                             reads=[("st2", tt)], writes=[("st2", tt)])
                        P.op("act", lambda e, c2=c2: e.activation(out=stat[:, c2], in_=stat[:, c2], func=AF.Exp, scale=-0.5),
                             reads=[("st2", tt)], writes=[("st2", tt)])
                        P.op("dve", lambda e, tt=tt, c2=c2: e.tensor_scalar(out=xn[:, tt, :], in0=X[:, tt, :], scalar1=stat[:, c2], scalar2=None,
                                                                            op0=ALU.mult),
                             reads=[("X", tt), ("st2", tt)], writes=[("xn", tt)])

                    def tr_tt(tt):
                        P.stage = 5.4
                        b = bank()
                        pst = ps[b][:].bitcast(BF)
                        for c in range(8):
                            P.op("pe", lambda e, tt=tt, c=c, pst=pst: e.transpose(out=pst[:, c * 128:(c + 1) * 128],
                                                                                  in_=xn[:, tt, c * 128:(c + 1) * 128], identity=ident_bf),
                                 reads=[("xn", tt), "cbf"], writes=[("ps", b)])
                        go = voff["g_fpre"]
                        P.op("dve", lambda e, tt=tt, pst=pst: e.tensor_tensor(
                            out=h2T[:, :, tt * 128:(tt + 1) * 128], in0=pst[:, 0:1024].rearrange("p (c t) -> p c t", c=8),
                            in1=vecs[:, go:go + 8].rearrange("p (c o) -> p c o", o=1).to_broadcast([128, 8, 128]), op=ALU.mult),
                            reads=[("ps", b), "vecs"], writes=[("h2T", tt)])

                    wo_tt(0)
                    epi_tt(0)
                    wo_tt(1)
                    epi_tt(1)
                    wo_tt(2)
                    tr_tt(0)
                    epi_tt(2)
                    wo_tt(3)
                    tr_tt(1)
                    epi_tt(3)
                    tr_tt(2)
                    tr_tt(3)
                    P.stage = 5.5
                    h2reads = [("h2T", tt) for tt in range(4)]
                    genA = phaseA_gen(nxt_blk[0], nxt_blk[1], xs0n, xn, junkA) if nxt_blk is not None else iter(())
                    for s_ in range(11):
                        next(genA, None)
                        sl = load_slab(15 + s_)
                        for jj in range(2):
                            fcn = s_ * 2 + jj
                            bg = bank()
                            P.mm_group([mm(ps[bg][:], slab[sl][:, kc * 512 + jj * 128: kc * 512 + (jj + 1) * 128], h2T[:, kc, :],
                                           kc == 0, kc == 7) for kc in range(8)],
                                       reads=[("slab", sl)] + h2reads, writes=[("ps", bg)])
                            bu = bank()
                            P.mm_group([mm(ps[bu][:], slab[sl][:, kc * 512 + 256 + jj * 128: kc * 512 + 256 + (jj + 1) * 128], h2T[:, kc, :],
                                           kc == 0, kc == 7) for kc in range(8)],
                                       reads=[("slab", sl)] + h2reads, writes=[("ps", bu)])
                            P.op("act", lambda e, bg=bg: e.activation(out=sg[:], in_=ps[bg][:], func=AF.Silu),
                                 reads=[("ps", bg)], writes=["m1"])
                            P.op("dve", lambda e, bu=bu, fcn=fcn: e.tensor_tensor(out=actT[:, fcn, :], in0=ps[bu][:], in1=sg[:], op=ALU.mult),
                                 reads=[("ps", bu), "m1"], writes=[("actT", fcn)])
                    areads = [("actT", c) for c in range(22)]
                    for hf in range(2):
                        bks = [bank() for _ in range(4)]
                        for si_, (k0, nk) in enumerate(((0, 8), (8, 8), (16, 6))):
                            next(genA, None)
                            sl = load_slab(26 + hf * 3 + si_)
                            for tt in range(4):
                                b = bks[tt]
                                P.mm_group([mm(ps[b][:], actT[:, k0 + kc, tt * 128:(tt + 1) * 128], slab[sl][:, kc * 512:(kc + 1) * 512],
                                               (k0 + kc) == 0, (k0 + kc) == 21) for kc in range(nk)],
                                           reads=[("slab", sl)] + areads, writes=[("ps", b)])
                        for tt in range(4):
                            b = bks[tt]
                            P.op("act", lambda e, b=b, tt=tt, hf=hf: e.activation(out=mo[:, tt, hf * 512:(hf + 1) * 512], in_=ps[b][:], func=AF.Copy),
                                 reads=[("ps", b)], writes=[("mo", tt)])
                    for _ in genA:
                        pass
                    post_norm_residual(mo, 12, 1024, junk, pnt, X)
                    P.stage = 1
                    P.dma("pool", y_d[b_, t0:t0 + BT, :].rearrange("(t p) d -> p t d", p=128), X[:], "y",
                          reads=[("X", tt) for tt in range(4)])
                    P.fence()
                P.fence()

        P.final_wait_all_dma("sp")
        P.final_wait_all_dma("pool")

        with nc.Block() as block:
            @block.sync
            def _(e):
                P.replay("sp", e)

            @block.tensor
            def _(e):
                P.replay("pe", e)

            @block.scalar
            def _(e):
                P.replay("act", e)

            @block.vector
            def _(e):
                P.replay("dve", e)

            @block.gpsimd
            def _(e):
                P.replay("pool", e)
    return nc, consts_np


def make_inputs(inputs, core, nseq, consts_np):
    g = lambda k: np.asarray(inputs[k], np.float32)[0]
    voff, nvec = _vec_offs()
    vec = np.zeros((128, nvec), np.float32)
    vec[:, voff["g_pre"]:voff["g_pre"] + 8] = _fm(g("norm_mix_pre"), 8)
    vec[:, voff["g_fpre"]:voff["g_fpre"] + 8] = _fm(g("norm_ffn_pre"), 8)
    vec[:, voff["b_gate"]:voff["b_gate"] + 16] = _fm(g("b_gate"), 16)
    vec[:, voff["mu"]:voff["mu"] + 14] = _fm(g("mu_rw"), 14)
    for k in ("w0", "a0", "k_k", "k_a", "lnx_w", "lnx_b"):
        vec[:, voff[k]:voff[k] + 4] = _fm(g(k), 4)
    vec[:, voff["r_k"]:voff["r_k"] + 4] = _fm(g("r_k").reshape(-1), 4)
    rows = np.ascontiguousarray(np.broadcast_to(
        np.concatenate([g("norm_mix_post"), g("norm_ffn_post")])[None, :], (128, 2048)))
    x = np.asarray(inputs["x"], np.float32)
    return {
        "x": np.ascontiguousarray(x[core * nseq:(core + 1) * nseq]),
        "w_in": g("w_in"), "w_sb_out": g("w_sb_out"), "w_rw_out": g("w_rw_out"), "w_o": g("w_o"),
        "w_gate": g("w_ffn_gate"), "w_upf": g("w_ffn_up"), "w_down": g("w_ffn_down"),
        "lora_wa": np.ascontiguousarray(np.concatenate([g("w_up"), g("a_up")], axis=0)),
        "lora_g": g("g_up"),
        "consts": consts_np, "vecs": vec, "rows": rows,
    }


def kernel(**inputs):
    x = np.asarray(inputs["x"])
    B, S, _ = x.shape
    nseq = B // NCORES
    nc, consts_np = build(nseq, S)
    in_maps = [make_inputs(inputs, c, nseq, consts_np) for c in range(NCORES)]
    res = run_bass_kernel_spmd(nc, in_maps, core_ids=list(range(NCORES)))
    return np.concatenate([r["y"] for r in res.results], axis=0).astype(np.float32)
```

```python
import contextlib
import numpy as np
import concourse.bass as bass
import concourse.mybir as mybir
from concourse.bass_utils import run_bass_kernel_spmd

F32 = mybir.dt.float32
BF = mybir.dt.bfloat16
AF = mybir.ActivationFunctionType
ALU = mybir.AluOpType
AX = mybir.AxisListType

D = 1024
BT = 512
NCORES = 8
C0 = float(np.exp(-0.5))
EPOCH = 4000
PAIRMODE = 0


class _Rec:
    def __init__(self):
        self.call = None

    def __getattr__(self, name):
        def f(*a, **k):
            self.call = (name, a, k)
            return self
        return f


def _bind(fn):
    r = _Rec()
    fn(r)
    name, a, k = r.call
    return lambda e: getattr(e, name)(*a, **k)


class Prog:
    def __init__(self, nc, es):
        self.nc = nc
        self.es = es
        self.eng = {"pe": nc.tensor, "act": nc.scalar, "dve": nc.vector, "pool": nc.gpsimd, "sp": nc.sync}
        self.stream = {e: [] for e in self.eng}
        self.cnt = {e: 0 for e in self.eng}
        self.sems = {}
        self.dmacnt = {}
        self.lastw = {}
        self.readers = {}
        self.wabs = {e: {} for e in self.eng}
        self.wdma = {e: {} for e in self.eng}
        self.stage = 0
        self.maxstage = 99

    def sem(self, key):
        if key not in self.sems:
            self.sems[key] = self.es.enter_context(self.nc.semaphore("s_" + "_".join(map(str, key))))
        return self.sems[key]

    def _waits(self, eng, reads, writes):
        deps = []
        for k in reads:
            w = self.lastw.get(k)
            if w is not None:
                deps.append((w, "raw"))
        for k in writes:
            w = self.lastw.get(k)
            if w is not None:
                deps.append((w, "waw"))
            for r in self.readers.get(k, {}).values():
                deps.append((r, "war"))
        need_e = {}
        need_d = {}
        for pt, kind in deps:
            if pt[0] == "e":
                _, src, n = pt
                if src == eng and eng == "pe":
                    continue
                if self.wabs[eng].get(src, 0) >= n:
                    continue
                need_e[src] = max(need_e.get(src, 0), n)
            else:
                _, name, val = pt
                if self.wdma[eng].get(name, 0) >= val:
                    continue
                need_d[name] = max(need_d.get(name, 0), val)
        waits = []
        for src, n in need_e.items():
            self.wabs[eng][src] = n
            ep = (n - 1) // EPOCH
            waits.append((self.sem(("e", src, ep)), (n - 1) % EPOCH + 1))
        for name, val in need_d.items():
            self.wdma[eng][name] = val
            waits.append((self.sem(("d", name)), val))
        return waits

    def _record(self, pt, rkey, reads, writes):
        for k in reads:
            self.readers.setdefault(k, {})[rkey] = pt
        for k in writes:
            self.lastw[k] = pt
            self.readers[k] = {}

    def op(self, eng, fn, reads=(), writes=()):
        if self.stage > self.maxstage:
            return
        waits = self._waits(eng, reads, writes)
        self.cnt[eng] += 1
        n = self.cnt[eng]
        ep = (n - 1) // EPOCH
        self.stream[eng].append((waits, _bind(fn), (self.sem(("e", eng, ep)), 1)))
        self._record(("e", eng, n), eng, reads, writes)

    def mm_group(self, fns, reads=(), writes=()):
        eng = "pe"
        if self.stage > self.maxstage:
            return
        waits = self._waits(eng, reads, writes)
        self.cnt[eng] += 1
        n = self.cnt[eng]
        ep = (n - 1) // EPOCH
        for i, fn in enumerate(fns):
            last = i == len(fns) - 1
            self.stream[eng].append((waits if i == 0 else [], _bind(fn), (self.sem(("e", eng, ep)), 1) if last else None))
        self._record(("e", eng, n), eng, reads, writes)

    def dma(self, eng, out, in_, semname, reads=(), writes=()):
        if self.stage > self.maxstage:
            return
        waits = self._waits(eng, reads, writes)
        self.dmacnt[semname] = self.dmacnt.get(semname, 0) + 16
        val = self.dmacnt[semname]
        self.stream[eng].append((waits, lambda e: e.dma_start(out=out, in_=in_), (self.sem(("d", semname)), 16)))
        self._record(("d", semname, val), "d_" + semname, reads, writes)

    def final_wait_all_dma(self, eng):
        waits = []
        for name, val in self.dmacnt.items():
            waits.append((self.sem(("d", name)), val))
        self.stream[eng].append((waits, None, None))

    def fence(self, engs=("pe", "act", "dve", "pool"), dmas=()):
        pts = {e: self.cnt[e] for e in engs}
        for e in engs:
            if e == "pe":
                continue
            waits = []
            for name in dmas:
                val = self.dmacnt.get(name, 0)
                if val and self.wdma[e].get(name, 0) < val:
                    self.wdma[e][name] = val
                    waits.append((self.sem(("d", name)), val))
            for src, n in pts.items():
                if src == e or n == 0:
                    continue
                if self.wabs[e].get(src, 0) >= n:
                    continue
                self.wabs[e][src] = n
                ep = (n - 1) // EPOCH
                waits.append((self.sem(("e", src, ep)), (n - 1) % EPOCH + 1))
            if waits:
                self.stream[e].append((waits, None, None))

    def replay(self, eng, e):
        for waits, fn, sig in self.stream[eng]:
            for s, v in waits:
                e.wait_ge(s, v)
            if fn is None:
                continue
            ins = fn(e)
            if sig is not None:
                ins.then_inc(sig[0], sig[1])


def _consts():
    p = np.arange(128)
    c = {}
    c["ident"] = np.eye(128, dtype=np.float32)
    c["trineg"] = -(p[:, None] >= p[None, :]).astype(np.float32)
    c["onesneg"] = -np.ones((128, 128), np.float32)
    t = np.arange(512)
    am = np.zeros((128, 4, 512), np.float32)
    for i in range(4):
        tt = t // 128
        m = np.where(tt[None, :] < i, 0.0, np.where(tt[None, :] == i, (p[:, None] < (t % 128)[None, :]).astype(np.float32), 1.0))
        am[:, i, :] = m
    c["amask"] = am.reshape(128, 2048)
    c["blockones"] = (p[:, None] // 64 == p[None, :] // 64).astype(np.float32)
    same = (p[:, None] // 64 == p[None, :] // 64)
    maskS = (same & (p[:, None] < p[None, :])).astype(np.float32)
    maskI = (same & (p[:, None] <= p[None, :])).astype(np.float32)
    c["mask4"] = np.concatenate([maskS, maskI, maskS, maskI], axis=1)
    c["masklt"] = np.ascontiguousarray(maskS.T)
    c["cm01"] = np.stack([(p < 64), (p >= 64)], axis=1).astype(np.float32)
    c["cmask"] = np.broadcast_to((t % 64 != 0).astype(np.float32)[None, :], (128, 512)).copy()
    offs = {}
    cols = []
    o = 0
    for k, v in c.items():
        offs[k] = (o, v.shape[1])
        cols.append(v)
        o += v.shape[1]
    return np.ascontiguousarray(np.concatenate(cols, axis=1)), offs


_VEC_SPEC = [("g_pre", 8), ("g_fpre", 8), ("b_gate", 16), ("mu", 14), ("w0", 4), ("a0", 4), ("k_k", 4),
             ("k_a", 4), ("r_k", 4), ("lnx_w", 4), ("lnx_b", 4)]


def _vec_offs():
    offs = {}
    o = 0
    for k, n in _VEC_SPEC:
        offs[k] = o
        o += n
    return offs, o


def _fm(v, n):
    return np.ascontiguousarray(np.asarray(v, np.float32).reshape(n, 128).T)


def _slab_table():
    sl = []
    for j in range(3):
        sl.append(("w_in", 0, 8, j * 512, 512))
    for j in range(4):
        c0 = 1536 + j * 512
        sl.append(("w_in", 0, 8, c0, min(512, 3328 - c0)))
    for j in range(4):
        sl.append(("w_in", 0, 8, 3328 + j * 512, 512))
    sl.append(("w_sb_out", 0, 4, 0, 1024))
    sl.append(("w_rw_out", 0, 4, 0, 1024))
    for j in range(2):
        sl.append(("w_o", 0, 8, j * 512, 512))
    for j in range(11):
        sl.append(("w_gu", 0, 8, j * 256, 256))
    for hf in range(2):
        for (k0, nk) in ((0, 8), (8, 8), (16, 6)):
            sl.append(("w_down", k0, nk, hf * 512, 512))
    return sl


def build(nseq, S, dbg=False, maxstage=99):
    nblk = S // BT
    consts_np, coff = _consts()
    voff, nvec = _vec_offs()
    slabs = _slab_table()
    NSL = len(slabs)
    NCC = consts_np.shape[1]
    nc = bass.Bass("TRN2", target_bir_lowering=False)
    dt = nc.dram_tensor
    x_d = dt("x", [nseq, S, D], F32, kind="ExternalInput").ap()
    y_d = dt("y", [nseq, S, D], F32, kind="ExternalOutput").ap()
    wd = {
        "w_in": dt("w_in", [1024, 5376], F32, kind="ExternalInput").ap(),
        "w_sb_out": dt("w_sb_out", [512, 1024], F32, kind="ExternalInput").ap(),
        "w_rw_out": dt("w_rw_out", [512, 1024], F32, kind="ExternalInput").ap(),
        "w_o": dt("w_o", [1024, 1024], F32, kind="ExternalInput").ap(),
        "w_gate": dt("w_gate", [1024, 2816], F32, kind="ExternalInput").ap(),
        "w_upf": dt("w_upf", [1024, 2816], F32, kind="ExternalInput").ap(),
        "w_down": dt("w_down", [2816, 1024], F32, kind="ExternalInput").ap(),
    }
    lw_d = dt("lora_wa", [128, 512], F32, kind="ExternalInput").ap()
    lg_d = dt("lora_g", [128, 512], F32, kind="ExternalInput").ap()
    consts_d = dt("consts", [128, NCC], F32, kind="ExternalInput").ap()
    vecs_d = dt("vecs", [128, nvec], F32, kind="ExternalInput").ap()
    rows_d = dt("rows", [128, 2048], F32, kind="ExternalInput").ap()
    wsl_d = dt("wsl", [NSL, 128, 4096], BF, kind="Internal").ap()
    dbg_d = dt("dbg", [128, 4096], F32, kind="ExternalOutput").ap() if dbg else None

    es = contextlib.ExitStack()
    uid = [0]
    with es:
        P = Prog(nc, es)
        P.maxstage = maxstage

        def sbx(stack, name, shape, dtype):
            uid[0] += 1
            return stack.enter_context(nc.sbuf_tensor(f"sb{uid[0]}_{name}", shape, dtype))

        def sb(name, shape, dtype):
            return sbx(es, name, shape, dtype)

        cbf = sb("cbf", [128, NCC], BF)
        vecs = sb("vecs", [128, nvec], F32)
        rows = sb("rows", [128, 2048], F32)
        onem = sb("onem", [128, 32], F32)
        bones_f = sb("bones_f", [128, 128], F32)
        lwa_bf = sb("lwa_bf", [128, 512], BF)
        lg_bf = sb("lg_bf", [128, 512], BF)
        stat = sb("stat", [128, 32], F32)
        hT = sb("hT", [128, 8, 512], BF)
        kT = sb("kT", [128, 4, S], BF)
        vtm = sb("vtm", [128, S // 128, 512], BF)
        osbT = sb("osbT", [128, 4, 512], BF)
        orwT = sb("orwT", [128, 4, 512], BF)
        carry = sb("carry", [128, 16], F32)
        zeros_bf = sb("zeros_bf", [128, 64], BF)
        S32 = sb("S32", [128, 4, 64], F32)
        slab = [sb(f"slab{i}", [128, 4096], BF) for i in range(4)]
        psall = es.enter_context(nc.psum_tensor("psall", [128, 8, 512], F32))
        ps = [psall[:, i, :] for i in range(8)]

        def cs(name):
            o, n = coff[name]
            return cbf[:, o:o + n]

        ident_bf = cs("ident")
        trineg_bf = cs("trineg")
        onesneg_bf = cs("onesneg")
        bones_bf = cs("blockones")
        mask4 = cs("mask4")
        masklt = cs("masklt")
        cmask = cs("cmask")
        cm01 = cs("cm01")

        def amask(i):
            o, n = coff["amask"]
            return cbf[:, o + i * 512:o + (i + 1) * 512]

        def vcol(name, c):
            o = voff[name] + c
            return vecs[:, o:o + 1]

        for c0_ in range(0, NCC, 1024):
            c1_ = min(NCC, c0_ + 1024)
            P.dma("pool", cbf[:, c0_:c1_], consts_d[:, c0_:c1_], "cst", writes=["cbf"])
        P.dma("sp", vecs[:], vecs_d, "cvec", writes=["vecs"])
        P.dma("sp", rows[:], rows_d, "crow", writes=["rows"])
        P.dma("pool", lwa_bf[:], lw_d, "clwa", writes=["lwa"])
        P.dma("pool", lg_bf[:], lg_d, "clg", writes=["lg"])
        P.op("dve", lambda e: e.tensor_copy(out=bones_f[:], in_=bones_bf), reads=["cbf"], writes=["bones_f"])
        P.op("dve", lambda e: e.memset(zeros_bf[:], 0.0), writes=["zeros"])
        P.op("dve", lambda e: e.tensor_scalar(out=onem[:, 0:14], in0=vecs[:, voff["mu"]:voff["mu"] + 14], scalar1=-1.0,
                                              scalar2=1.0, op0=ALU.mult, op1=ALU.add), reads=["vecs"], writes=["onem"])
        P.op("dve", lambda e: e.tensor_scalar(out=onem[:, 16:20], in0=vecs[:, voff["k_a"]:voff["k_a"] + 4], scalar1=-1.0,
                                              scalar2=1.0, op0=ALU.mult, op1=ALU.add), reads=["vecs"], writes=["onem"])
        for si, (wn, k0, nk, c0, ncol) in enumerate(slabs):
            dst = wsl_d[si]
            if wn == "w_gu":
                for half, src in enumerate((wd["w_gate"], wd["w_upf"])):
                    o = dst.rearrange("p (k n) -> p k n", k=8)[:, :, half * 256:(half + 1) * 256]
                    i = src.rearrange("(k p) n -> p k n", p=128)[:, :, c0:c0 + ncol]
                    P.dma("pool", o, i, f"cw{si}", writes=[("wsl", si)])
            else:
                width = 4096 // nk if wn != "w_down" else 512
                o = dst[:, 0:nk * width].rearrange("p (k n) -> p k n", k=nk)[:, :, 0:ncol]
                i = wd[wn].rearrange("(k p) n -> p k n", p=128)[:, k0:k0 + nk, c0:c0 + ncol]
                P.dma("pool", o, i, f"cw{si}", writes=[("wsl", si)])

        slab_i = [0]

        def load_slab(si):
            slot = slab_i[0] % 4
            slab_i[0] += 1
            wn, k0, nk, c0, ncol = slabs[si]
            width = 512 if wn in ("w_down", "w_gu") else 4096 // nk
            used = 512 if wn == "w_gu" else ncol
            if used == width:
                P.dma("sp", slab[slot][:, 0:nk * width], wsl_d[si][:, 0:nk * width], f"sl{slot}",
                      reads=[("wsl", si)], writes=[("slab", slot)])
            else:
                P.dma("sp", slab[slot][:, 0:nk * width].rearrange("p (k n) -> p k n", k=nk)[:, :, 0:used],
                      wsl_d[si][:, 0:nk * width].rearrange("p (k n) -> p k n", k=nk)[:, :, 0:used], f"sl{slot}",
                      reads=[("wsl", si)], writes=[("slab", slot)])
            return slot

        psi = [0]

        def bank(lo=0, hi=8):
            b = lo + psi[0] % (hi - lo)
            psi[0] += 1
            return b

        def mm(out, lhsT, rhs, start=True, stop=True):
            return lambda e: e.matmul(out, lhsT=lhsT, rhs=rhs, start=start, stop=stop)

        def rstd_from_ss(col0, eps):
            P.op("act", lambda e: e.activation(out=stat[:, col0:col0 + 4], in_=stat[:, col0:col0 + 4], func=AF.Ln,
                                               scale=1.0 / D, bias=eps),
                 reads=[("stat", col0)], writes=[("stat", col0)])
            P.op("act", lambda e: e.activation(out=stat[:, col0:col0 + 4], in_=stat[:, col0:col0 + 4], func=AF.Exp, scale=-0.5),
                 reads=[("stat", col0)], writes=[("stat", col0)])

        def norm_transpose(src, skey, col0, gname, dstT, dkey, xn, junk):
            for tt in range(4):
                P.op("act", lambda e, tt=tt: e.activation(out=junk[:], in_=src[:, tt, :], func=AF.Square,
                                                          accum_out=stat[:, col0 + tt:col0 + tt + 1]),
                     reads=[(skey, tt)], writes=["junk", ("stat", col0)])
            rstd_from_ss(col0, 1e-6)
            for tt in range(4):
                P.op("dve", lambda e, tt=tt: e.tensor_scalar(out=xn[:, tt, :], in0=src[:, tt, :],
                                                             scalar1=stat[:, col0 + tt:col0 + tt + 1], scalar2=None,
                                                             op0=ALU.mult),
                     reads=[(skey, tt), ("stat", col0)], writes=[("xn", tt)])
            for c in range(8):
                b = bank()
                pst = ps[b][:].bitcast(BF)
                for tt in range(4):
                    P.op("pe", lambda e, tt=tt, c=c, pst=pst: e.transpose(out=pst[:, tt * 128:(tt + 1) * 128],
                                                                          in_=xn[:, tt, c * 128:(c + 1) * 128],
                                                                          identity=ident_bf),
                         reads=[("xn", tt), "cbf"], writes=[("ps", b)])
                P.op("act", lambda e, c=c, pst=pst: e.activation(out=dstT[:, c, :], in_=pst[:, 0:512], func=AF.Copy,
                                                                 scale=vcol(gname, c)),
                     reads=[("ps", b), "vecs"], writes=[(dkey, c)])

        def phaseA_gen(b_, blk_, xs0, xn, junk):
            t0_ = blk_ * BT
            P.dma("pool", xs0[:], x_d[b_, t0_:t0_ + BT, :].rearrange("(t p) d -> p t d", p=128), "x0",
                  writes=[("xs0", tt) for tt in range(4)])
            yield
            for tt in range(4):
                P.op("act", lambda e, tt=tt: e.activation(out=junk[:], in_=xs0[:, tt, :], func=AF.Square,
                                                          accum_out=stat[:, tt:tt + 1]),
                     reads=[("xs0", tt)], writes=["junkA", ("stat", 0)])
            rstd_from_ss(0, 1e-6)
            yield
            for tt in range(4):
                P.op("dve", lambda e, tt=tt: e.tensor_scalar(out=xn[:, tt, :], in0=xs0[:, tt, :],
                                                             scalar1=stat[:, tt:tt + 1], scalar2=None, op0=ALU.mult),
                     reads=[("xs0", tt), ("stat", 0)], writes=[("xn", tt)])
                if tt % 2 == 1:
                    yield
            for c in range(8):
                b = bank()
                pst = ps[b][:].bitcast(BF)
                for tt in range(4):
                    P.op("pe", lambda e, tt=tt, c=c, pst=pst: e.transpose(out=pst[:, tt * 128:(tt + 1) * 128],
                                                                          in_=xn[:, tt, c * 128:(c + 1) * 128],
                                                                          identity=ident_bf),
                         reads=[("xn", tt), "cbf"], writes=[("ps", b)])
                P.op("act", lambda e, c=c, pst=pst: e.activation(out=hT[:, c, :], in_=pst[:, 0:512], func=AF.Copy,
                                                                 scale=vcol("g_pre", c)),
                     reads=[("ps", b), "vecs"], writes=[("hT", c)])
                yield

        def post_norm_residual(src, col0, rowoff, junk, tmp, X):
            for tt in range(4):
                P.op("act", lambda e, tt=tt: e.activation(out=junk[:], in_=src[:, tt, :], func=AF.Square,
                                                          accum_out=stat[:, col0 + tt:col0 + tt + 1]),
                     reads=[("mo", tt)], writes=["junk", ("stat", col0)])
            rstd_from_ss(col0, 1e-6)
            for tt in range(4):
                P.op("dve", lambda e, tt=tt: e.scalar_tensor_tensor(out=tmp[:], in0=src[:, tt, :],
                                                                    scalar=stat[:, col0 + tt:col0 + tt + 1],
                                                                    in1=rows[:, rowoff:rowoff + 1024],
                                                                    op0=ALU.mult, op1=ALU.mult),
                     reads=[("mo", tt), ("stat", col0), "rows"], writes=["pn_tmp"])
                P.op("dve", lambda e, tt=tt: e.tensor_tensor(out=X[:, tt, :], in0=X[:, tt, :], in1=tmp[:], op=ALU.add),
                     reads=["pn_tmp", ("X", tt)], writes=[("X", tt)])

        for b_ in range(nseq):
            for blk in range(nblk):
                t0 = blk * BT
                nkt = 4 * blk + 4
                ph = contextlib.ExitStack()
                with ph:
                    def pb(name, shape, dtype):
                        return sbx(ph, name, shape, dtype)
                    pm = pb("pm", [128, 14, 512], BF)
                    P.stage = 1
                    if b_ == 0 and blk == 0:
                        sa = contextlib.ExitStack()
                        with sa:
                            xs0 = sbx(sa, "xs0", [128, 4, 1024], F32)
                            xn = sbx(sa, "xn", [128, 4, 1024], BF)
                            junk = sbx(sa, "junk", [128, 1024], BF)
                            for _ in phaseA_gen(b_, blk, xs0, xn, junk):
                                pass
                            P.fence()
                    sc = contextlib.ExitStack()
                    sc.__enter__()
                    qT = sbx(sc, "qT", [128, 4, 512], BF)
                    pmt = [sbx(sc, f"pmt{i}", [128, 512], F32) for i in range(2)]
                    Ebuf = [sbx(sc, f"E{i}", [128, 2, 512], F32) for i in range(2)]
                    SPb = [sbx(sc, f"SP{i}", [128, 2, 512], BF) for i in range(2)]
                    Wtb = [sbx(sc, f"Wt{i}", [128, 2, 512], BF) for i in range(2)]
                    lsum = [[[sbx(sc, f"lsum{i}_{q}_{j}", [128, 512], BF) for j in range(2)] for q in range(2)] for i in range(2)]
                    hreads = [("hT", c) for c in range(8)]
                    P.stage = 2
                    sl = load_slab(0)
                    for j in range(4):
                        b = bank()
                        P.mm_group([mm(ps[b][:], slab[sl][:, kc * 512 + j * 128: kc * 512 + (j + 1) * 128], hT[:, kc, :],
                                       kc == 0, kc == 7) for kc in range(8)],
                                   reads=[("slab", sl)] + hreads, writes=[("ps", b)])
                        P.op("act", lambda e, j=j, b=b: e.activation(out=qT[:, j, :], in_=ps[b][:], func=AF.Copy, scale=0.125),
                             reads=[("ps", b)], writes=[("qT", j)])
                    sl = load_slab(1)
                    for j in range(4):
                        b = bank()
                        P.mm_group([mm(ps[b][:], slab[sl][:, kc * 512 + j * 128: kc * 512 + (j + 1) * 128], hT[:, kc, :],
                                       kc == 0, kc == 7) for kc in range(8)],
                                   reads=[("slab", sl)] + hreads, writes=[("ps", b)])
                        P.op("dve", lambda e, j=j, b=b, t0=t0: e.tensor_copy(out=kT[:, j, t0:t0 + BT], in_=ps[b][:]),
                             reads=[("ps", b)], writes=[("kT", j)])
                    sl = load_slab(2)
                    for tt in range(4):
                        b = bank()
                        P.mm_group([mm(ps[b][:], hT[:, kc, tt * 128:(tt + 1) * 128], slab[sl][:, kc * 512:(kc + 1) * 512],
                                       kc == 0, kc == 7) for kc in range(8)],
                                   reads=[("slab", sl)] + hreads, writes=[("ps", b)])
                        P.op("act", lambda e, tt=tt, b=b, blk=blk: e.activation(out=vtm[:, blk * 4 + tt, :], in_=ps[b][:], func=AF.Copy),
                             reads=[("ps", b)], writes=[("vtm", blk * 4 + tt)])
                    if blk == 0:
                        P.op("dve", lambda e: e.memset(carry[:], 0.0), writes=["carry"])
                        P.op("dve", lambda e: e.memset(S32[:], 0.0), writes=["S32"])
                    sls = {}
                    for c in range(14):
                        if c % 4 == 0:
                            sls[c // 4] = load_slab(3 + c // 4)
                        sl = sls[c // 4]
                        j = c % 4
                        b = bank()
                        P.mm_group([mm(ps[b][:], slab[sl][:, kc * 512 + j * 128: kc * 512 + (j + 1) * 128], hT[:, kc, :],
                                       kc == 0, kc == 7) for kc in range(8)],
                                   reads=[("slab", sl)] + hreads, writes=[("ps", b)])
                        tm_ = pmt[c % 2]
                        P.op("act", lambda e, c=c, b=b, tm_=tm_: e.activation(out=tm_[:], in_=ps[b][:], func=AF.Copy,
                                                                              scale=onem[:, c:c + 1]),
                             reads=[("ps", b), "onem"], writes=[("pmt", c % 2)])
                        P.op("dve", lambda e, c=c, b=b, tm_=tm_: e.scalar_tensor_tensor(out=pm[:, c, 1:512], in0=ps[b][:, 0:511],
                                                                                        scalar=vcol("mu", c), in1=tm_[:, 1:512],
                                                                                        op0=ALU.mult, op1=ALU.add),
                             reads=[("ps", b), ("pmt", c % 2), "vecs"], writes=[("pm", c)])
                        P.op("dve", lambda e, c=c, tm_=tm_: e.scalar_tensor_tensor(out=pm[:, c, 0:1], in0=carry[:, c:c + 1],
                                                                                   scalar=vcol("mu", c), in1=tm_[:, 0:1],
                                                                                   op0=ALU.mult, op1=ALU.add),
                             reads=["carry", ("pmt", c % 2), "vecs"], writes=[("pm", c)])
                        P.op("dve", lambda e, c=c, b=b: e.tensor_copy(out=carry[:, c:c + 1], in_=ps[b][:, 511:512]),
                             reads=[("ps", b)], writes=["carry"])

                    P.stage = 3
                    trimask = amask(0)[:, 0:128]
                    units = []
                    for hp in range(4):
                        for kb in range(nkt - 1, -1, -1):
                            for hh in range(2):
                                units.append(dict(hp=hp, hh=hh, kb=kb, i=kb - 4 * blk, first=(kb == nkt - 1), last=(kb == 0),
                                                  n=len(units)))
                    par = {0: 0, 1: 0}

                    def stA(u):
                        hp, hh, kb, n = u["hp"], u["hh"], u["kb"], u["n"]
                        base = hh * 64
                        c0_ = max(u["i"], 0) * 128
                        cols = slice(c0_, 512)
                        u["cols"] = cols
                        bz = 2 * ((n // 2) % 2) + hh
                        u["kap"] = kT[base:base + 64, hp, kb * 128:(kb + 1) * 128]
                        u["qap"] = qT[base:base + 64, hp, cols]
                        if u["first"]:
                            for li_, l_ in enumerate(lsum[hh][hp % 2]):
                                P.op("pool", lambda e, l_=l_: e.memset(l_[:], 0.0), writes=[("lsum", hh, hp % 2), ("lsum", hh, hp % 2, li_)])
                            bo = 6 + hp % 2
                            P.op("pe", mm(ps[bo][base:base + 64, :], zeros_bf[:, 0:64], amask(1), True, False),
                                 reads=["zeros", "cbf"], writes=[("ps", bo)])
                        P.mm_group([mm(ps[bz][:, cols], u["kap"], u["qap"])], reads=[("kT", hp), ("qT", hp)], writes=[("ps", bz)])

                    def stA2(ua):
                        p_ = ua["n"] // 2
                        cols = ua["cols"]
                        zb = 2 * (p_ % 2)
                        Ep, SPp = Ebuf[p_ % 2], SPb[p_ % 2]
                        P.op("act", lambda e, Ep=Ep, zb=zb, cols=cols: e.activation(out=Ep[:, :, cols], in_=psall[:, zb:zb + 2, cols], func=AF.Exp),
                             reads=[("ps", zb), ("ps", zb + 1)], writes=[("E", p_ % 2)])
                        P.op("act", lambda e, Ep=Ep, SPp=SPp, cols=cols: e.activation(out=SPp[:, :, cols], in_=Ep[:, :, cols], func=AF.Ln, bias=1.0),
                             reads=[("E", p_ % 2)], writes=[("SP", p_ % 2)])
                        if ua["i"] >= 0:
                            dsl = slice(cols.start, cols.start + 128)
                            P.op("pool", lambda e, SPp=SPp, dsl=dsl: e.tensor_tensor(
                                out=SPp[:, :, dsl], in0=SPp[:, :, dsl],
                                in1=trimask.rearrange("p (a t) -> p a t", a=1).to_broadcast([128, 2, 128]), op=ALU.mult),
                                 reads=[("SP", p_ % 2), "cbf"], writes=[("SP", p_ % 2)])

                    def stB(u):
                        hp, hh, kb, n = u["hp"], u["hh"], u["kb"], u["n"]
                        cols = u["cols"]
                        SPt = SPb[(n // 2) % 2][:, hh, :]
                        Wt = Wtb[(n // 2) % 2][:, hh, :]
                        bw = 4 + hh
                        cur = lsum[hh][hp % 2][par[hh]]
                        nxt = lsum[hh][hp % 2][1 - par[hh]]
                        grp = [mm(ps[bw][:, cols], u["kap"], u["qap"], True, False),
                               mm(ps[bw][:, cols], trineg_bf, SPt[:, cols], False, u["first"])]
                        rd = [("kT", hp), ("qT", hp), ("SP", (n // 2) % 2), "cbf"]
                        if not u["first"]:
                            grp.append(mm(ps[bw][:, cols], onesneg_bf, cur[:, cols], False, True))
                            rd.append(("lsum", hh, hp % 2, par[hh]))
                        u["grpB"] = (grp, rd, bw)

                    def stB2(u):
                        hp, hh, kb, n = u["hp"], u["hh"], u["kb"], u["n"]
                        cols = u["cols"]
                        SPt = SPb[(n // 2) % 2][:, hh, :]
                        Wt = Wtb[(n // 2) % 2][:, hh, :]
                        bw = 4 + hh
                        cur = lsum[hh][hp % 2][par[hh]]
                        nxt = lsum[hh][hp % 2][1 - par[hh]]
                        if not u["last"]:
                            P.op("dve", lambda e, cur=cur, nxt=nxt, SPt=SPt, cols=cols: e.tensor_tensor(out=nxt[:, cols], in0=cur[:, cols], in1=SPt[:, cols], op=ALU.add),
                                 reads=[("SP", (n // 2) % 2), ("lsum", hh, hp % 2, par[hh]), ("lsum", hh, hp % 2)], writes=[("lsum", hh, hp % 2, 1 - par[hh])])
                            par[hh] = 1 - par[hh]
                        else:
                            par[hh] = 0

                    def stB3(ua):
                        p_ = ua["n"] // 2
                        cols = ua["cols"]
                        Wp = Wtb[p_ % 2]
                        P.op("act", lambda e, Wp=Wp, cols=cols: e.activation(out=Wp[:, :, cols], in_=psall[:, 4:6, cols], func=AF.Exp),
                             reads=[("ps", 4), ("ps", 5)], writes=[("Wt", p_ % 2)])
                        if ua["i"] >= 0:
                            dsl = slice(cols.start, cols.start + 128)
                            P.op("pool", lambda e, Wp=Wp, dsl=dsl: e.tensor_tensor(
                                out=Wp[:, :, dsl], in0=Wp[:, :, dsl],
                                in1=trimask.rearrange("p (a t) -> p a t", a=1).to_broadcast([128, 2, 128]), op=ALU.mult),
                                 reads=[("Wt", p_ % 2), "cbf"], writes=[("Wt", p_ % 2)])

                    def stC(u):
                        hp, hh, kb, n = u["hp"], u["hh"], u["kb"], u["n"]
                        base = hh * 64
                        cols = u["cols"]
                        Wt = Wtb[(n // 2) % 2][:, hh, :]
                        bo = 6 + hp % 2
                        h = hp * 2 + hh
                        P.op("pe", mm(ps[bo][base:base + 64, cols], vtm[:, kb, h * 64:(h + 1) * 64], Wt[:, cols], False, u["last"]),
                             reads=[("Wt", (n // 2) % 2), ("vtm", kb)], writes=[("ps", bo)])
                        if u["last"] and hh == 1:
                            P.op("act", lambda e, hp=hp, bo=bo: e.activation(out=osbT[:, hp, :], in_=ps[bo][:], func=AF.Copy),
                                 reads=[("ps", bo)], writes=[("osbT", hp)])

                    NU = len(units)
                    NP = NU // 2
                    for it in range(NP + 2):
                        if it < NP:
                            stA(units[2 * it])
                            stA(units[2 * it + 1])
                            stA2(units[2 * it])
                        if 0 <= it - 2 < NP and PAIRMODE == 1:
                            stC(units[2 * (it - 2)])
                        if 0 <= it - 1 < NP:
                            ua, ub = units[2 * (it - 1)], units[2 * (it - 1) + 1]
                            stB(ua)
                            stB(ub)
                            (ga, ra, bwa), (gb, rb_, bwb) = ua["grpB"], ub["grpB"]
                            P.mm_group([ga[0], gb[0]] + ga[1:] + gb[1:], reads=ra + rb_, writes=[("ps", bwa), ("ps", bwb)])
                            stB2(ua)
                            stB2(ub)
                            stB3(ua)
                        if 0 <= it - 2 < NP:
                            if PAIRMODE != 1:
                                stC(units[2 * (it - 2)])
                            stC(units[2 * (it - 2) + 1])
                    P.fence(dmas=("y",))
                    sc.__exit__(None, None, None)
                    P.stage = 4
                    rw = contextlib.ExitStack()
                    with rw:
                        def rb(name, shape, dtype):
                            return sbx(rw, name, shape, dtype)
                        lin = rb("lin", [128, 512], BF)
                        sgx = rb("sgx", [128, 512], BF)
                        sigw = rb("sigw", [128, 512], F32)
                        aa = rb("aa", [128, 512], F32)
                        sq = rb("sq", [128, 512], BF)
                        kk = rb("kk", [128, 512], F32)
                        kmod = rb("kmod", [128, 512], F32)
                        bb = rb("bb", [128, 512], F32)
                        Lp = rb("Lp", [128, 512], F32)
                        Dd = rb("Dd", [128, 512], F32)
                        Dd2 = rb("Dd2", [128, 512], F32)
                        e1 = rb("e1", [128, 512], F32)
                        e2 = rb("e2", [128, 512], F32)
                        e3 = rb("e3", [128, 512], F32)
                        gamC = rb("gamC", [128, 8], F32)
                        AR = rb("AR", [128, 4, 2, 128], BF)
                        BK = rb("BK", [128, 4, 2, 128], BF)
                        vbf = rb("vbf", [128, 512], BF)
                        rkr = rb("rkr", [128, 512], BF)
                        TM4 = rb("TM4", [128, 4, 4, 128], BF)
                        TMb4 = rb("TMb4", [128, 4, 2, 2, 128], BF)
                        MMs4g = [rb(f"MMs4_{g}", [128, 4, 512], BF) for g in range(2)]
                        PP4g = [[rb(f"PP4_{g}_{l}", [128, 4, 256], BF) for l in range(5)] for g in range(2)]
                        PT54g = [rb(f"PT54_{g}", [128, 4, 128], BF) for g in range(2)]
                        Z4g = [[rb(f"Z4_{g}_{i}", [128, 4, 128], BF) for i in range(2)] for g in range(2)]
                        U0b4g = [rb(f"U0b4_{g}", [128, 4, 2, 64], BF) for g in range(2)]
                        RhT = rb("RhT", [128, 512], BF)
                        QT = rb("QT", [128, 8, 128], BF)
                        Sbr = [rb(f"Sbr{i}", [128, 64], F32) for i in range(2)]
                        Xs = rb("Xs", [128, 64], F32)
                        gsh = rb("gsh", [128, 8], F32)
                        Sbf = rb("Sbf", [128, 64], BF)
                        Sbd = rb("Sbd", [128, 128], BF)
                        gamCs = [gamC, rb("gamC1", [128, 8], F32), rb("gamC2", [128, 8], F32)]
                        rkrs = [rkr, rb("rkr1", [128, 512], BF), rb("rkr2", [128, 512], BF)]
                        Y0T = rb("Y0T", [128, 512], F32)
                        Hs = rb("Hs", [128, 512], F32)
                        YT = rb("YT", [128, 512], F32)
                        cenP = rb("cenP", [128, 512], F32)
                        tmpP = rb("tmpP", [128, 512], F32)
                        t1, rn = Dd2, e3
                        cen, sqc, rs, bonus, gT = cenP, tmpP, tmpP, tmpP, tmpP

                        P.op("pool", lambda e: e.memset(QT[:], 0.0), writes=["QT"])
                        P.op("pool", lambda e: e.memset(Sbd[:], 0.0), writes=["Sbd"])
                        P.op("act", lambda e: e.activation(out=lin[0:64, :], in_=pm[0:64, 12, :], func=AF.Tanh),
                             reads=[("pm", 12)], writes=["lin0"])
                        P.op("act", lambda e: e.activation(out=lin[64:128, :], in_=pm[64:128, 12, :], func=AF.Copy),
                             reads=[("pm", 12)], writes=["lin1"])
                        P.op("act", lambda e: e.activation(out=sgx[:], in_=pm[:, 13, :], func=AF.Sigmoid),
                             reads=[("pm", 13)], writes=["sgx"])
                        ucnt = [0]
                        def prep_gen(fc):
                            P.stage = 4
                            R = pm[:, fc, :]
                            Kr = pm[:, 4 + fc, :]
                            V = pm[:, 8 + fc, :]
                            rk = [("pm", fc), ("pm", 4 + fc), ("pm", 8 + fc)]
                            fsl = slice(fc * 128, (fc + 1) * 128)
                            b = bank(0, 4)
                            P.mm_group([mm(ps[b][:], lwa_bf[0:64, fsl], lin[0:64, :])], reads=["lwa", "lin0"], writes=[("ps", b)])
                            P.op("act", lambda e, b=b, fc=fc: e.activation(out=sigw[:], in_=ps[b][:], func=AF.Sigmoid, bias=vcol("w0", fc)),
                                 reads=[("ps", b), "vecs"], writes=["sigw"])
                            b = bank(0, 4)
                            P.mm_group([mm(ps[b][:], lwa_bf[64:128, fsl], lin[64:128, :])], reads=["lwa", "lin1"], writes=[("ps", b)])
                            P.op("act", lambda e, b=b, fc=fc: e.activation(out=aa[:], in_=ps[b][:], func=AF.Sigmoid, bias=vcol("a0", fc)),
                                 reads=[("ps", b), "vecs"], writes=["aa"])
                            yield
                            P.op("act", lambda e, fc=fc, Kr=Kr: e.activation(out=sq[:], in_=Kr, func=AF.Square, scale=vcol("k_k", fc)),
                                 reads=rk + ["vecs"], writes=["sq"])
                            b = bank(0, 4)
                            P.mm_group([mm(ps[b][:], bones_bf, sq[:])], reads=["cbf", "sq"], writes=[("ps", b)])
                            P.op("act", lambda e, b=b: e.activation(out=rn[:], in_=ps[b][:], func=AF.Ln, bias=1e-24),
                                 reads=[("ps", b)], writes=["e3"])
                            P.op("act", lambda e: e.activation(out=rn[:], in_=rn[:], func=AF.Exp, scale=-0.5),
                                 reads=["e3"], writes=["e3"])
                            P.op("dve", lambda e, fc=fc, Kr=Kr: e.scalar_tensor_tensor(out=kk[:], in0=Kr, scalar=vcol("k_k", fc), in1=rn[:],
                                                                                       op0=ALU.mult, op1=ALU.mult),
                                 reads=rk + ["e3", "vecs"], writes=["kk"])
                            yield
                            P.op("dve", lambda e, fc=fc: e.tensor_scalar(out=t1[:], in0=aa[:], scalar1=vcol("k_a", fc),
                                                                         scalar2=onem[:, 16 + fc:17 + fc], op0=ALU.mult, op1=ALU.add),
                                 reads=["aa", "vecs", "onem"], writes=["Dd2"])
                            P.op("dve", lambda e, Kr=Kr: e.tensor_tensor(out=kmod[:], in0=Kr, in1=t1[:], op=ALU.mult),
                                 reads=rk + ["Dd2"], writes=["kmod"])
                            P.op("pool", lambda e: e.tensor_tensor(out=bb[:], in0=kk[:], in1=aa[:], op=ALU.mult),
                                 reads=["kk", "aa"], writes=["bb"])
                            yield
                            P.stage = 4.1
                            P.op("dve", lambda e: e.tensor_tensor_scan(out=Lp[:], data0=cmask, data1=sigw[:], initial=0.0,
                                                                       op0=ALU.mult, op1=ALU.add),
                                 reads=["cbf", "sigw"], writes=["Lp"])
                            Lp3 = Lp[:].rearrange("p (c t) -> p c t", t=64)
                            P.op("dve", lambda e, Lp3=Lp3: e.tensor_tensor(out=Dd[:].rearrange("p (c t) -> p c t", t=64),
                                                                           in0=Lp3[:, :, 63:64].to_broadcast([128, 8, 64]),
                                                                           in1=Lp3, op=ALU.subtract),
                                 reads=["Lp"], writes=["Dd"])
                            P.op("pool", lambda e: e.tensor_tensor(out=Dd2[:], in0=Dd[:], in1=sigw[:], op=ALU.add),
                                 reads=["Dd", "sigw"], writes=["Dd2"])
                            yield
                            P.op("act", lambda e: e.activation(out=e1[:], in_=Dd[:], func=AF.Exp, scale=C0), reads=["Dd"], writes=["e1"])
                            P.op("act", lambda e: e.activation(out=e2[:], in_=Dd[:], func=AF.Exp, scale=-C0), reads=["Dd"], writes=["e2"])
                            P.op("act", lambda e: e.activation(out=e3[:], in_=Dd2[:], func=AF.Exp, scale=C0), reads=["Dd2"], writes=["e3"])
                            yield
                            P.op("act", lambda e, Lp3=Lp3: e.activation(out=gamCs[fc % 3][:].rearrange("p (c o) -> p c o", o=1), in_=Lp3[:, :, 63:64],
                                                                        func=AF.Exp, scale=-C0),
                                 reads=["Lp"], writes=[("gamC", fc % 3)])
                            P.op("act", lambda e, V=V: e.activation(out=vbf[:], in_=V, func=AF.Copy), reads=rk, writes=["vbf"])
                            P.op("dve", lambda e, fc=fc, R=R: e.scalar_tensor_tensor(out=rkrs[fc % 3][:], in0=R, scalar=vcol("r_k", fc), in1=kmod[:],
                                                                                     op0=ALU.mult, op1=ALU.mult),
                                 reads=rk + ["kmod", "vecs"], writes=[("rkr", fc % 3)])
                            yield "SPLIT"
                            P.stage = 4.2
                            v4 = lambda t: t[:].rearrange("p (c t) -> p c t", t=128)
                            P.op("dve", lambda e: e.scalar_tensor_tensor(out=AR[:, :, 0, :], in0=v4(kk), scalar=-1.0, in1=v4(e3),
                                                                         op0=ALU.mult, op1=ALU.mult),
                                 reads=["kk", "e3"], writes=["AR"])
                            P.op("pool", lambda e, R=R: e.tensor_tensor(out=AR[:, :, 1, :], in0=R.rearrange("p (c t) -> p c t", t=128),
                                                                        in1=v4(e1), op=ALU.mult),
                                 reads=rk + ["e1"], writes=["AR"])
                            yield
                            P.op("dve", lambda e: e.tensor_tensor(out=BK[:, :, 0, :], in0=v4(bb), in1=v4(e2), op=ALU.mult),
                                 reads=["bb", "e2"], writes=["BK"])
                            P.op("pool", lambda e: e.tensor_tensor(out=BK[:, :, 1, :], in0=v4(kmod), in1=v4(e2), op=ALU.mult),
                                 reads=["kmod", "e2"], writes=["BK"])
                            yield
                            yield
                            yield "SPLIT2"
                            P.stage = 4.3
                            for cp2 in range(2):
                                b = bank(0, 4)
                                pst = ps[b][:].bitcast(BF)
                                for ci in range(2):
                                    cp = cp2 * 2 + ci
                                    srcs = [AR[:, cp, 0, :], BK[:, cp, 0, :], BK[:, cp, 1, :], vbf[:, cp * 128:(cp + 1) * 128]]
                                    for qi, sap in enumerate(srcs):
                                        o_ = ci * 512 + qi * 128
                                        P.op("pe", lambda e, o_=o_, sap=sap, pst=pst: e.transpose(out=pst[:, o_:o_ + 128], in_=sap, identity=ident_bf),
                                             reads=["AR", "BK", "vbf", "cbf"], writes=[("ps", b)])
                                P.op("act", lambda e, cp2=cp2, pst=pst: e.activation(
                                    out=TM4[:, cp2 * 2:cp2 * 2 + 2].rearrange("p c a b -> p (c a b)"), in_=pst[:, 0:1024], func=AF.Copy),
                                     reads=[("ps", b)], writes=["TM4"])
                            for c2 in range(2):
                                for qd, qs in ((0, 1), (1, 3)):
                                    P.op("dve", lambda e, c2=c2, qd=qd, qs=qs: e.tensor_scalar(
                                        out=TMb4[:, :, c2, qd, :], in0=TM4[:, :, qs, :], scalar1=cm01[:, c2:c2 + 1], scalar2=None, op0=ALU.mult),
                                        reads=["TM4", "cbf"], writes=["TMb4"])
                            yield
                        def groups(fc):
                            hs = None
                            P.stage = 4.4
                            def grp(hh, MMs4, PP4, PT54, Z4, U0b4, gk, rot):
                                hs = slice(hh * 64, hh * 64 + 64)
                                for cp in range(4):
                                    arf = AR[hs, cp].rearrange("p a t -> p (a t)")
                                    P.mm_group([mm(ps[cp][:, 0:256], BK[hs, cp, 0, :], arf)], reads=["AR", "BK"], writes=[("ps", cp)])
                                    P.mm_group([mm(ps[cp][:, 256:512], BK[hs, cp, 1, :], arf)], reads=["AR", "BK"], writes=[("ps", cp)])
                                    if cp % 2 == 1:
                                        P.op("dve", lambda e, cp=cp: e.tensor_tensor(
                                            out=MMs4[:, cp - 1:cp + 1, :], in0=psall[:, cp - 1:cp + 1, :],
                                            in1=mask4.rearrange("p (a t) -> p a t", a=1).to_broadcast([128, 2, 512]), op=ALU.mult),
                                             reads=[("ps", cp - 1), ("ps", cp), "cbf"], writes=[("MMs4", gk)])
                                yield
                                for cp in range(4):
                                    P.mm_group([mm(ps[0][:, cp * 128:(cp + 1) * 128], AR[hs, cp, 0, :], BK[hs, cp, 0, :])],
                                               reads=["AR", "BK"], writes=[("ps", 0)])
                                P.op("dve", lambda e: e.tensor_tensor(out=PP4[0][:, :, 0:128], in0=ps[0][:].rearrange("p (c t) -> p c t", c=4),
                                                                      in1=masklt.rearrange("p (a t) -> p a t", a=1).to_broadcast([128, 4, 128]), op=ALU.mult),
                                     reads=[("ps", 0), "cbf"], writes=[("PP4", gk, 0)])
                                for cp in range(4):
                                    P.mm_group([mm(ps[2][:, cp * 64:(cp + 1) * 64], MMs4[:, cp, 256:384], TM4[:, cp, 3, hs])],
                                               reads=[("MMs4", gk), "TM4"], writes=[("ps", 2)])
                                P.op("act", lambda e, hs=hs: e.activation(out=Z4[0][:, :, 0:64], in_=TM4[:, :, 0, hs], func=AF.Copy),
                                     reads=["TM4"], writes=[("Z4", gk, 0)])
                                P.op("act", lambda e: e.activation(out=Z4[0][:, :, 64:128], in_=ps[2][:, 0:256].rearrange("p (c t) -> p c t", c=4), func=AF.Copy),
                                     reads=[("ps", 2)], writes=[("Z4", gk, 0)])
                                yield
                                P.stage = 4.5
                                bsel = [0]

                                def nb():
                                    b_ = (1, 3, 0, 2)[(bsel[0] + rot) % 4]
                                    bsel[0] += 1
                                    return b_
                                for l in range(6):
                                    zi, zo = Z4[l % 2], Z4[(l + 1) % 2]
                                    pair = None
                                    if l < 4:
                                        pair = (nb(), nb())
                                        for cp in range(4):
                                            bk = pair[cp // 2]
                                            off = (cp % 2) * 256
                                            Pl = PP4[l][:, cp, 0:128]
                                            PTl = PP4[l][:, cp, 128:256] if l > 0 else MMs4[:, cp, 0:128]
                                            P.mm_group([mm(ps[bk][:, off:off + 128], PTl, Pl)], reads=[("PP4", gk, l), ("MMs4", gk)], writes=[("ps", bk)])
                                            P.mm_group([mm(ps[bk][:, off + 128:off + 256], Pl, PTl)], reads=[("PP4", gk, l), ("MMs4", gk)], writes=[("ps", bk)])
                                    elif l == 4:
                                        b5 = nb()
                                        for cp in range(4):
                                            P.mm_group([mm(ps[b5][:, cp * 128:(cp + 1) * 128], PP4[4][:, cp, 0:128], PP4[4][:, cp, 128:256])],
                                                       reads=[("PP4", gk, 4)], writes=[("ps", b5)])
                                    ba = nb()
                                    for cp in range(4):
                                        PTl = (PP4[l][:, cp, 128:256] if l > 0 else MMs4[:, cp, 0:128]) if l < 5 else PT54[:, cp, :]
                                        P.mm_group([mm(ps[ba][:, cp * 128:(cp + 1) * 128], PTl, zi[:, cp, :])],
                                                   reads=[("PP4", gk, l) if l < 5 else ("PT54", gk), ("MMs4", gk), ("Z4", gk, l % 2)], writes=[("ps", ba)])
                                    if l < 4:
                                        for half in range(2):
                                            bk = pair[half]
                                            if half == 0 or gk == 1:
                                                P.op("act", lambda e, l=l, half=half, bk=bk: e.activation(
                                                    out=PP4[l + 1][:, 2 * half:2 * half + 2, :].rearrange("p c t -> p (c t)"), in_=ps[bk][:], func=AF.Copy),
                                                    reads=[("ps", bk)], writes=[("PP4", gk, l + 1)])
                                            else:
                                                P.op("dve", lambda e, l=l, half=half, bk=bk: e.tensor_copy(
                                                    out=PP4[l + 1][:, 2 * half:2 * half + 2, :].rearrange("p c t -> p (c t)"), in_=ps[bk][:]),
                                                    reads=[("ps", bk)], writes=[("PP4", gk, l + 1)])
                                    elif l == 4:
                                        P.op("act", lambda e, b5=b5: e.activation(out=PT54[:].rearrange("p c t -> p (c t)"), in_=ps[b5][:], func=AF.Copy),
                                             reads=[("ps", b5)], writes=[("PT54", gk)])
                                    P.op("dve", lambda e, zi=zi, zo=zo, ba=ba: e.tensor_tensor(
                                        out=zo[:].rearrange("p c t -> p (c t)"), in0=ps[ba][:], in1=zi[:].rearrange("p c t -> p (c t)"), op=ALU.add),
                                        reads=[("ps", ba), ("Z4", gk, l % 2)], writes=[("Z4", gk, (l + 1) % 2)])
                                    yield
                                P.stage = 4.6
                                Zf = Z4[0]
                                for c2 in range(2):
                                    P.op("dve", lambda e, Zf=Zf, c2=c2: e.tensor_scalar(
                                        out=U0b4[:, :, c2, :], in0=Zf[:, :, 64:128], scalar1=cm01[:, c2:c2 + 1], scalar2=None, op0=ALU.mult),
                                        reads=[("Z4", gk, 0), "cbf"], writes=[("U0b4", gk)])
                                zk = [("Z4", gk, 0), ("MMs4", gk), "TM4", "TMb4", ("U0b4", gk)]
                                for cp in range(4):
                                    csl = slice(cp * 128, (cp + 1) * 128)
                                    MrbT, MrkT = MMs4[:, cp, 128:256], MMs4[:, cp, 384:512]
                                    Vtm = TM4[:, cp, 3, hs]
                                    P.mm_group([mm(ps[4][hs, csl], Zf[:, cp, 64:128], MrbT, True, False),
                                                mm(ps[4][hs, csl], Vtm, MrkT, False, True)], reads=zk, writes=[("ps", 4)])
                                    P.mm_group([mm(ps[5][hs, csl], Zf[:, cp, 0:64], MrbT)], reads=zk, writes=[("ps", 5)])
                                    o6 = ps[6][hs, csl].rearrange("p (a b) -> p a b", a=2)
                                    o7 = ps[7][hs, csl].rearrange("p (a b) -> p a b", a=2)
                                    P.mm_group([mm(o6, Zf[:, cp, 0:64], TMb4[:, cp, :, 0, hs])], reads=zk, writes=[("ps", 6)])
                                    P.mm_group([mm(o7, TM4[:, cp, 1, hs], U0b4[:, cp], True, False),
                                                mm(o7, TM4[:, cp, 2, hs], TMb4[:, cp, :, 1, hs], False, True)],
                                               reads=zk, writes=[("ps", 7)])
                            gens = [grp(0, MMs4g[0], PP4g[0], PT54g[0], Z4g[0], U0b4g[0], 0, 0), grp(1, MMs4g[1], PP4g[1], PT54g[1], Z4g[1], U0b4g[1], 1, 2)]
                            live = list(gens)
                            while live:
                                for g_ in list(live):
                                    try:
                                        next(g_)
                                    except StopIteration:
                                        live.remove(g_)
                                yield
                            P.stage = 4.7
                            P.op("act", lambda e: e.activation(out=Y0T[:], in_=ps[4][:], func=AF.Copy), reads=[("ps", 4)], writes=["Y0T"])
                            P.op("dve", lambda e: e.tensor_tensor(out=RhT[:].rearrange("p (c t) -> p c t", t=128),
                                                                  in0=ps[5][:].rearrange("p (c t) -> p c t", t=128),
                                                                  in1=AR[:, :, 1, :], op=ALU.add),
                                 reads=[("ps", 5), "AR"], writes=["RhT"])
                            for hh in range(2):
                                hs = slice(hh * 64, hh * 64 + 64)
                                P.op("act", lambda e, hh=hh, hs=hs: e.activation(out=QT[hs, :, hh * 64:(hh + 1) * 64],
                                                                                 in_=ps[6][hs, :].rearrange("p (c k) -> p c k", k=64), func=AF.Copy),
                                     reads=[("ps", 6)], writes=["QT"])
                            P.op("dve", lambda e: e.tensor_copy(out=Hs[:], in_=ps[7][:]), reads=[("ps", 7)], writes=["Hs"])
                        def tail_gen(fc):
                            R = pm[:, fc, :]
                            Kr = pm[:, 4 + fc, :]
                            V = pm[:, 8 + fc, :]
                            rk = [("pm", fc), ("pm", 4 + fc), ("pm", 8 + fc)]
                            fsl = slice(fc * 128, (fc + 1) * 128)
                            P.stage = 4.8
                            P.op("dve", lambda e: e.tensor_copy(out=gsh[:, 0:7], in_=gamCs[fc % 3][:, 1:8]), reads=[("gamC", fc % 3)], writes=["gsh"])
                            P.op("dve", lambda e: e.memset(gsh[:, 7:8], 1.0), writes=["gsh"])
                            P.op("dve", lambda e: e.tensor_tensor(out=Hs[:].rearrange("p (c v) -> p c v", v=64),
                                                                  in0=Hs[:].rearrange("p (c v) -> p c v", v=64),
                                                                  in1=gsh[:].rearrange("p (c o) -> p c o", o=1).to_broadcast([128, 8, 64]), op=ALU.mult),
                                 reads=["Hs", "gsh"], writes=["Hs"])
                            P.op("dve", lambda e, fc=fc: e.tensor_scalar(out=Sbr[0][:], in0=S32[:, fc, :], scalar1=gamCs[fc % 3][:, 0:1],
                                                                         scalar2=None, op0=ALU.mult),
                                 reads=["S32", ("gamC", fc % 3)], writes=[("Sbr", 0)])
                            for c in range(8):
                                cc = slice(c * 64, c * 64 + 64)
                                sb_c = Sbr[c % 2]
                                P.op("act", lambda e, sb_c=sb_c: e.activation(out=Sbf[:], in_=sb_c[:], func=AF.Copy), reads=[("Sbr", c % 2)], writes=["Sbf"])
                                for hh in range(2):
                                    hs = slice(hh * 64, hh * 64 + 64)
                                    P.op("pool", lambda e, hh=hh, hs=hs, sb_c=sb_c: e.tensor_copy(out=Sbd[hs, hh * 64:(hh + 1) * 64], in_=sb_c[hs, :]),
                                         reads=[("Sbr", c % 2)], writes=["Sbd"])
                                bs = bank(0, 4)
                                P.mm_group([mm(ps[bs][:, 0:64], QT[:, c, :], Sbf[:])], reads=["QT", "Sbf"], writes=[("ps", bs)])
                                P.mm_group([mm(ps[bs][:, 64:128], Sbd[:], RhT[:, cc])], reads=["RhT", "Sbd"], writes=[("ps", bs)])
                                P.op("dve", lambda e, c=c, cc=cc, sb_c=sb_c: e.scalar_tensor_tensor(out=Xs[:], in0=sb_c[:], scalar=gsh[:, c:c + 1], in1=Hs[:, cc],
                                                                                                    op0=ALU.mult, op1=ALU.add),
                                     reads=[("Sbr", c % 2), "gsh", "Hs"], writes=["Xs"])
                                dst = Sbr[(c + 1) % 2][:] if c < 7 else S32[:, fc, :]
                                P.op("dve", lambda e, c=c, bs=bs, dst=dst: e.scalar_tensor_tensor(out=dst, in0=ps[bs][:, 0:64], scalar=gsh[:, c:c + 1], in1=Xs[:],
                                                                                                  op0=ALU.mult, op1=ALU.add),
                                     reads=[("ps", bs), "gsh", "Xs"], writes=[("Sbr", (c + 1) % 2) if c < 7 else "S32"])
                                P.op("dve", lambda e, bs=bs, cc=cc: e.tensor_tensor(out=YT[:, cc], in0=ps[bs][:, 64:128], in1=Y0T[:, cc], op=ALU.add),
                                     reads=[("ps", bs), "Y0T"], writes=["YT"])
                                yield
                            P.stage = 4.9
                            b = bank(0, 4)
                            P.mm_group([mm(ps[b][:], bones_f[:], YT[:])], reads=["bones_f", "YT"], writes=[("ps", b)])
                            P.op("dve", lambda e, b=b: e.scalar_tensor_tensor(out=cen[:], in0=ps[b][:], scalar=-1.0 / 64, in1=YT[:],
                                                                              op0=ALU.mult, op1=ALU.add),
                                 reads=[("ps", b), "YT"], writes=["cenP"])
                            P.op("act", lambda e: e.activation(out=sqc[:], in_=cen[:], func=AF.Square), reads=["cenP"], writes=["tmpP"])
                            yield
                            b = bank(0, 4)
                            P.mm_group([mm(ps[b][:], bones_f[:], sqc[:])], reads=["bones_f", "tmpP"], writes=[("ps", b)])
                            P.op("act", lambda e, b=b: e.activation(out=rs[:], in_=ps[b][:], func=AF.Ln, scale=1.0 / 64, bias=64e-5),
                                 reads=[("ps", b)], writes=["tmpP"])
                            P.op("act", lambda e: e.activation(out=rs[:], in_=rs[:], func=AF.Exp, scale=-0.5),
                                 reads=["tmpP"], writes=["tmpP"])
                            P.op("dve", lambda e: e.tensor_tensor(out=cen[:], in0=cen[:], in1=rs[:], op=ALU.mult),
                                 reads=["cenP", "tmpP"], writes=["cenP"])
                            P.op("dve", lambda e, fc=fc: e.tensor_scalar(out=cen[:], in0=cen[:], scalar1=vcol("lnx_w", fc),
                                                                         scalar2=vcol("lnx_b", fc), op0=ALU.mult, op1=ALU.add),
                                 reads=["cenP", "vecs"], writes=["cenP"])
                            b = bank(0, 4)
                            P.mm_group([mm(ps[b][:], bones_bf, rkrs[fc % 3][:])], reads=["cbf", ("rkr", fc % 3)], writes=[("ps", b)])
                            P.op("dve", lambda e, b=b, V=V: e.tensor_tensor(out=bonus[:], in0=ps[b][:], in1=V, op=ALU.mult),
                                 reads=[("ps", b), ("pm", 8 + fc)], writes=["tmpP"])
                            P.op("dve", lambda e: e.tensor_tensor(out=cen[:], in0=cen[:], in1=bonus[:], op=ALU.add),
                                 reads=["cenP", "tmpP"], writes=["cenP"])
                            b = bank(0, 4)
                            P.mm_group([mm(ps[b][:], lg_bf[:, fsl], sgx[:])], reads=["lg", "sgx"], writes=[("ps", b)])
                            P.op("act", lambda e, b=b: e.activation(out=gT[:], in_=ps[b][:], func=AF.Copy),
                                 reads=[("ps", b)], writes=["tmpP"])
                            P.op("dve", lambda e, fc=fc: e.tensor_tensor(out=orwT[:, fc, :], in0=cen[:], in1=gT[:], op=ALU.mult),
                                 reads=["cenP", "tmpP"], writes=[("orwT", fc)])
                            yield
                        def drive(*gens):
                            live = list(gens)
                            while live:
                                for g_ in list(live):
                                    try:
                                        next(g_)
                                    except StopIteration:
                                        live.remove(g_)

                        def drive3(gmain, gtail, gprep, gpre=None):
                            live = [g_ for g_ in (gpre, gmain, gtail, gprep) if g_ is not None]
                            parked = False
                            while live:
                                for g_ in list(live):
                                    if g_ is gprep and parked:
                                        if len(live) == 1:
                                            return
                                        continue
                                    try:
                                        r_ = next(g_)
                                        if g_ is gprep and r_ == "SPLIT":
                                            parked = True
                                    except StopIteration:
                                        live.remove(g_)

                        def drive_until(g_, tag):
                            while True:
                                try:
                                    if next(g_) == tag:
                                        return
                                except StopIteration:
                                    return
                        gcur = prep_gen(0)
                        drive_until(gcur, "SPLIT2")
                        for fc in range(4):
                            gp = prep_gen(fc + 1) if fc < 3 else None
                            drive3(groups(fc), tail_gen(fc - 1) if fc > 0 else None, gp, gcur)
                            if gp is not None:
                                drive_until(gp, "SPLIT2")
                            gcur = gp
                        drive(tail_gen(3))
                    P.fence()
                P.fence()
                P.stage = 5
                ph = contextlib.ExitStack()
                with ph:
                    def pb(name, shape, dtype):
                        return sbx(ph, name, shape, dtype)
                    actT = pb("actT", [128, 22, 512], BF)
                    mo = pb("mo", [128, 4, 1024], F32)
                    mergedT = pb("mergedT", [128, 8, 512], BF)
                    X = pb("X", [128, 4, 1024], F32)
                    P.dma("pool", X[:], x_d[b_, t0:t0 + BT, :].rearrange("(t p) d -> p t d", p=128), "x",
                          writes=[("X", tt) for tt in range(4)])
                    xn = pb("xn", [128, 4, 1024], BF)
                    junk = pb("junk", [128, 1024], BF)
                    h2T = pb("h2T", [128, 8, 512], BF)
                    g1 = pb("g1", [128, 512], F32)
                    g2 = pb("g2", [128, 512], F32)
                    m1 = pb("m1", [128, 512], F32)
                    sg = m1
                    pnt = pb("pnt", [128, 1024], F32)
                    nxt_blk = (b_, blk + 1) if blk + 1 < nblk else ((b_ + 1, 0) if b_ + 1 < nseq else None)
                    if nxt_blk is not None:
                        xs0n = pb("xs0n", [128, 4, 1024], F32)
                        junkA = pb("junkA", [128, 1024], BF)
                    hreads = [("hT", c) for c in range(8)]
                    gsl = {}
                    sl_sb = sl_rw = None
                    for j in range(8):
                        if j % 4 == 0:
                            gsl[0] = load_slab(7 + j // 4)
                            gsl[1] = load_slab(9 + j // 4)
                            if j == 0:
                                sl_sb = load_slab(11)
                                sl_rw = load_slab(12)
                        jj = j % 4
                        for gi_, gt in ((0, g1), (1, g2)):
                            b = bank()
                            sl = gsl[gi_]
                            P.mm_group([mm(ps[b][:], slab[sl][:, kc * 512 + jj * 128: kc * 512 + (jj + 1) * 128], hT[:, kc, :],
                                           kc == 0, kc == 7) for kc in range(8)],
                                       reads=[("slab", sl)] + hreads, writes=[("ps", b)])
                            P.op("act", lambda e, b=b, gt=gt, col=gi_ * 8 + j: e.activation(out=gt[:], in_=ps[b][:], func=AF.Sigmoid,
                                                                                           bias=vcol("b_gate", col)),
                                 reads=[("ps", b), "vecs"], writes=[("g", gi_)])
                        b = bank()
                        P.mm_group([mm(ps[b][:], slab[sl_sb][:, kc * 1024 + j * 128: kc * 1024 + (j + 1) * 128], osbT[:, kc, :],
                                       kc == 0, kc == 3) for kc in range(4)],
                                   reads=[("slab", sl_sb)] + [("osbT", c) for c in range(4)], writes=[("ps", b)])
                        P.op("dve", lambda e, b=b: e.tensor_tensor(out=m1[:], in0=ps[b][:], in1=g1[:], op=ALU.mult),
                             reads=[("ps", b), ("g", 0)], writes=["m1"])
                        b = bank()
                        P.mm_group([mm(ps[b][:], slab[sl_rw][:, kc * 1024 + j * 128: kc * 1024 + (j + 1) * 128], orwT[:, kc, :],
                                       kc == 0, kc == 3) for kc in range(4)],
                                   reads=[("slab", sl_rw)] + [("orwT", c) for c in range(4)], writes=[("ps", b)])
                        P.op("dve", lambda e, b=b: e.tensor_tensor(out=g2[:], in0=ps[b][:], in1=g2[:], op=ALU.mult),
                             reads=[("ps", b), ("g", 1)], writes=[("g", 1)])
                        P.op("dve", lambda e, j=j: e.tensor_tensor(out=mergedT[:, j, :], in0=m1[:], in1=g2[:], op=ALU.add),
                             reads=["m1", ("g", 1)], writes=[("mergedT", j)])
                    mreads = [("mergedT", c) for c in range(8)]
                    sl_wo = [load_slab(13), load_slab(14)]

                    def wo_tt(tt):
                        P.stage = 5.1
                        for hf in range(2):
                            sl = sl_wo[hf]
                            b = bank()
                            P.mm_group([mm(ps[b][:], mergedT[:, kc, tt * 128:(tt + 1) * 128], slab[sl][:, kc * 512:(kc + 1) * 512],
                                           kc == 0, kc == 7) for kc in range(8)],
                                       reads=[("slab", sl)] + mreads, writes=[("ps", b)])
                            P.op("dve", lambda e, b=b, tt=tt, hf=hf: e.tensor_copy(out=mo[:, tt, hf * 512:(hf + 1) * 512], in_=ps[b][:]),
                                 reads=[("ps", b)], writes=[("mo", tt)])

                    def epi_tt(tt):
                        P.stage = 5.2
                        c1 = slice(4 + tt, 5 + tt)
                        c2 = slice(8 + tt, 9 + tt)
                        P.op("act", lambda e, tt=tt, c1=c1: e.activation(out=junk[:], in_=mo[:, tt, :], func=AF.Square, accum_out=stat[:, c1]),
                             reads=[("mo", tt)], writes=["junk", ("st1", tt)])
                        P.op("act", lambda e, c1=c1: e.activation(out=stat[:, c1], in_=stat[:, c1], func=AF.Ln, scale=1.0 / D, bias=1e-6),
                             reads=[("st1", tt)], writes=[("st1", tt)])
                        P.op("act", lambda e, c1=c1: e.activation(out=stat[:, c1], in_=stat[:, c1], func=AF.Exp, scale=-0.5),
                             reads=[("st1", tt)], writes=[("st1", tt)])
                        P.op("dve", lambda e, tt=tt, c1=c1: e.scalar_tensor_tensor(out=pnt[:], in0=mo[:, tt, :], scalar=stat[:, c1],
                                                                                   in1=rows[:, 0:1024], op0=ALU.mult, op1=ALU.mult),
                             reads=[("mo", tt), ("st1", tt), "rows"], writes=["pn_tmp"])
                        P.op("dve", lambda e, tt=tt: e.tensor_tensor(out=X[:, tt, :], in0=X[:, tt, :], in1=pnt[:], op=ALU.add),
                             reads=["pn_tmp", ("X", tt)], writes=[("X", tt)])
                        P.stage = 5.3
                        P.op("act", lambda e, tt=tt, c2=c2: e.activation(out=junk[:], in_=X[:, tt, :], func=AF.Square, accum_out=stat[:, c2]),
                             reads=[("X", tt)], writes=["junk", ("st2", tt)])
                        P.op("act", lambda e, c2=c2: e.activation(out=stat[:, c2], in_=stat[:, c2], func=AF.Ln, scale=1.0 / D, bias=1e-6),
                             reads=[("st2", tt)], writes=[("st2", tt)])
                        P.op("act", lambda e, c2=c2: e.activation(out=stat[:, c2], in_=stat[:, c2], func=AF.Exp, scale=-0.5),
                             reads=[("st2", tt)], writes=[("st2", tt)])
                        P.op("dve", lambda e, tt=tt, c2=c2: e.tensor_scalar(out=xn[:, tt, :], in0=X[:, tt, :], scalar1=stat[:, c2], scalar2=None,
                                                                            op0=ALU.mult),
                             reads=[("X", tt), ("st2", tt)], writes=[("xn", tt)])

                    def tr_tt(tt):
                        P.stage = 5.4
                        b = bank()
                        pst = ps[b][:].bitcast(BF)
                        for c in range(8):
                            P.op("pe", lambda e, tt=tt, c=c, pst=pst: e.transpose(out=pst[:, c * 128:(c + 1) * 128],
                                                                                  in_=xn[:, tt, c * 128:(c + 1) * 128], identity=ident_bf),
                                 reads=[("xn", tt), "cbf"], writes=[("ps", b)])
                        go = voff["g_fpre"]
                        P.op("dve", lambda e, tt=tt, pst=pst: e.tensor_tensor(
                            out=h2T[:, :, tt * 128:(tt + 1) * 128], in0=pst[:, 0:1024].rearrange("p (c t) -> p c t", c=8),
                            in1=vecs[:, go:go + 8].rearrange("p (c o) -> p c o", o=1).to_broadcast([128, 8, 128]), op=ALU.mult),
                            reads=[("ps", b), "vecs"], writes=[("h2T", tt)])

                    wo_tt(0)
                    epi_tt(0)
                    wo_tt(1)
                    epi_tt(1)
                    wo_tt(2)
                    tr_tt(0)
                    epi_tt(2)
                    wo_tt(3)
                    tr_tt(1)
                    epi_tt(3)
                    tr_tt(2)
                    tr_tt(3)
                    P.stage = 5.5
                    h2reads = [("h2T", tt) for tt in range(4)]
                    genA = phaseA_gen(nxt_blk[0], nxt_blk[1], xs0n, xn, junkA) if nxt_blk is not None else iter(())
                    for s_ in range(11):
                        next(genA, None)
                        sl = load_slab(15 + s_)
                        for jj in range(2):
                            fcn = s_ * 2 + jj
                            bg = bank()
                            P.mm_group([mm(ps[bg][:], slab[sl][:, kc * 512 + jj * 128: kc * 512 + (jj + 1) * 128], h2T[:, kc, :],
                                           kc == 0, kc == 7) for kc in range(8)],
                                       reads=[("slab", sl)] + h2reads, writes=[("ps", bg)])
                            bu = bank()
                            P.mm_group([mm(ps[bu][:], slab[sl][:, kc * 512 + 256 + jj * 128: kc * 512 + 256 + (jj + 1) * 128], h2T[:, kc, :],
                                           kc == 0, kc == 7) for kc in range(8)],
                                       reads=[("slab", sl)] + h2reads, writes=[("ps", bu)])
                            P.op("act", lambda e, bg=bg: e.activation(out=sg[:], in_=ps[bg][:], func=AF.Silu),
                                 reads=[("ps", bg)], writes=["m1"])
                            P.op("dve", lambda e, bu=bu, fcn=fcn: e.tensor_tensor(out=actT[:, fcn, :], in0=ps[bu][:], in1=sg[:], op=ALU.mult),
                                 reads=[("ps", bu), "m1"], writes=[("actT", fcn)])
                    areads = [("actT", c) for c in range(22)]
                    for hf in range(2):
                        bks = [bank() for _ in range(4)]
                        for si_, (k0, nk) in enumerate(((0, 8), (8, 8), (16, 6))):
                            next(genA, None)
                            sl = load_slab(26 + hf * 3 + si_)
                            for tt in range(4):
                                b = bks[tt]
                                P.mm_group([mm(ps[b][:], actT[:, k0 + kc, tt * 128:(tt + 1) * 128], slab[sl][:, kc * 512:(kc + 1) * 512],
                                               (k0 + kc) == 0, (k0 + kc) == 21) for kc in range(nk)],
                                           reads=[("slab", sl)] + areads, writes=[("ps", b)])
                        for tt in range(4):
                            b = bks[tt]
                            P.op("act", lambda e, b=b, tt=tt, hf=hf: e.activation(out=mo[:, tt, hf * 512:(hf + 1) * 512], in_=ps[b][:], func=AF.Copy),
                                 reads=[("ps", b)], writes=[("mo", tt)])
                    for _ in genA:
                        pass
                    post_norm_residual(mo, 12, 1024, junk, pnt, X)
                    P.stage = 1
                    P.dma("pool", y_d[b_, t0:t0 + BT, :].rearrange("(t p) d -> p t d", p=128), X[:], "y",
                          reads=[("X", tt) for tt in range(4)])
                    P.fence()
                P.fence()

        P.final_wait_all_dma("sp")
        P.final_wait_all_dma("pool")

        with nc.Block() as block:
            @block.sync
            def _(e):
                P.replay("sp", e)

            @block.tensor
            def _(e):
                P.replay("pe", e)

            @block.scalar
            def _(e):
                P.replay("act", e)

            @block.vector
            def _(e):
                P.replay("dve", e)

            @block.gpsimd
            def _(e):
                P.replay("pool", e)
    return nc, consts_np


def make_inputs(inputs, core, nseq, consts_np):
    g = lambda k: np.asarray(inputs[k], np.float32)[0]
    voff, nvec = _vec_offs()
    vec = np.zeros((128, nvec), np.float32)
    vec[:, voff["g_pre"]:voff["g_pre"] + 8] = _fm(g("norm_mix_pre"), 8)
    vec[:, voff["g_fpre"]:voff["g_fpre"] + 8] = _fm(g("norm_ffn_pre"), 8)
    vec[:, voff["b_gate"]:voff["b_gate"] + 16] = _fm(g("b_gate"), 16)
    vec[:, voff["mu"]:voff["mu"] + 14] = _fm(g("mu_rw"), 14)
    for k in ("w0", "a0", "k_k", "k_a", "lnx_w", "lnx_b"):
        vec[:, voff[k]:voff[k] + 4] = _fm(g(k), 4)
    vec[:, voff["r_k"]:voff["r_k"] + 4] = _fm(g("r_k").reshape(-1), 4)
    rows = np.ascontiguousarray(np.broadcast_to(
        np.concatenate([g("norm_mix_post"), g("norm_ffn_post")])[None, :], (128, 2048)))
    x = np.asarray(inputs["x"], np.float32)
    return {
        "x": np.ascontiguousarray(x[core * nseq:(core + 1) * nseq]),
        "w_in": g("w_in"), "w_sb_out": g("w_sb_out"), "w_rw_out": g("w_rw_out"), "w_o": g("w_o"),
        "w_gate": g("w_ffn_gate"), "w_upf": g("w_ffn_up"), "w_down": g("w_ffn_down"),
        "lora_wa": np.ascontiguousarray(np.concatenate([g("w_up"), g("a_up")], axis=0)),
        "lora_g": g("g_up"),
        "consts": consts_np, "vecs": vec, "rows": rows,
    }


def kernel(**inputs):
    x = np.asarray(inputs["x"])
    B, S, _ = x.shape
    nseq = B // NCORES
    nc, consts_np = build(nseq, S)
    in_maps = [make_inputs(inputs, c, nseq, consts_np) for c in range(NCORES)]
    res = run_bass_kernel_spmd(nc, in_maps, core_ids=list(range(NCORES)))
    return np.concatenate([r["y"] for r in res.results], axis=0).astype(np.float32)
```
